# Optimizing a Trainium2 kernel written in Bass

```python
import jax, jax.numpy as jnp
from jax import lax
import numpy as np

D_MODEL = 1024
BATCH = 2
SEQ = 16384
DEPTH = 4

N_MIXERS = 2
MEM_LEN = 256
HEAD_DIM = 64
MEM_HEADS = 4
MEM_WIDTH = MEM_HEADS * HEAD_DIM
TOK_WIDTH = D_MODEL - MEM_WIDTH
MOBA_HEADS = TOK_WIDTH // HEAD_DIM
MOBA_BLOCK = 256
MOBA_TOPK = 3
Q_CHUNK = 128
POOL_WINDOWS = (2, 4, 8, 16)
POOL_GROUPS = len(POOL_WINDOWS)
POOL_GROUP_WIDTH = TOK_WIDTH // POOL_GROUPS
D_FF = 4 * D_MODEL
ROPE_THETA = 10000.0
EPS = 1e-6
NEG = -1e30
N_POOL_LAYERS = (DEPTH + 1) // 2
N_MOBA_LAYERS = DEPTH // 2

kernel_name = "hybrid_pool_moba_memxattn_trunk"


def rms_norm(x, g):
    xf = x.astype(jnp.float32)
    y = xf * lax.rsqrt(jnp.mean(xf * xf, axis=-1, keepdims=True) + EPS)
    return (y * g.astype(jnp.float32)).astype(x.dtype)


def rope_tables(seq_len):
    pos = jnp.arange(seq_len, dtype=jnp.float32)
    inv = ROPE_THETA ** (-jnp.arange(0, HEAD_DIM, 2, dtype=jnp.float32) / HEAD_DIM)
    ang = pos[:, None] * inv[None, :]
    return jnp.cos(ang), jnp.sin(ang)


def apply_rope(x, cos, sin):
    x1, x2 = jnp.split(x, 2, axis=-1)
    c = cos.astype(x.dtype)
    s = sin.astype(x.dtype)
    return jnp.concatenate([x1 * c - x2 * s, x2 * c + x1 * s], axis=-1)


def causal_window_mean(u, w):
    S = u.shape[1]
    c = jnp.cumsum(u.astype(jnp.float32), axis=1)
    c = jnp.pad(c, ((0, 0), (1, 0), (0, 0)))
    hi = c[:, 1:]
    lo = jnp.pad(c[:, :S + 1 - w], ((0, 0), (w - 1, 0), (0, 0)))
    cnt = jnp.minimum(jnp.arange(S) + 1, w).astype(jnp.float32)[None, :, None]
    return (hi - lo) / cnt


def pool_mixer(u, w_group, scale):
    B, S, _ = u.shape
    ug = u.reshape(B, S, POOL_GROUPS, POOL_GROUP_WIDTH)
    pooled = jnp.stack([causal_window_mean(ug[:, :, g], w) for g, w in enumerate(POOL_WINDOWS)], axis=2)
    d = (pooled - ug.astype(jnp.float32)).astype(u.dtype)
    y = jnp.einsum('bsgc,gcd->bsgd', d, w_group)
    return y.reshape(B, S, TOK_WIDTH) * scale


def moba_mixer(qkv, q_gain, k_gain, cos, sin):
    B, S, _ = qkv.shape
    q, k, v = jnp.split(qkv, 3, axis=-1)

    def heads(t):
        return t.reshape(B, S, MOBA_HEADS, HEAD_DIM).transpose(0, 2, 1, 3)

    q = apply_rope(rms_norm(heads(q), q_gain), cos, sin)
    k = apply_rope(rms_norm(heads(k), k_gain), cos, sin)
    v = heads(v)
    H = MOBA_HEADS
    nb = -(-S // MOBA_BLOCK)
    pad = nb * MOBA_BLOCK - S
    kb = jnp.pad(k, ((0, 0), (0, 0), (0, pad), (0, 0))).reshape(B, H, nb, MOBA_BLOCK, HEAD_DIM)
    vb = jnp.pad(v, ((0, 0), (0, 0), (0, pad), (0, 0))).reshape(B, H, nb, MOBA_BLOCK, HEAD_DIM)
    k_mean = jnp.mean(kb.astype(jnp.float32), axis=3)
    topk = min(MOBA_TOPK, nb)
    n_chunks = S // Q_CHUNK
    scale = HEAD_DIM ** -0.5
    bi = jnp.arange(B)[:, None, None, None]
    hi = jnp.arange(H)[None, :, None, None]
    block_ids = jnp.arange(nb)
    key_off = jnp.arange(MOBA_BLOCK)
    q_off = jnp.arange(Q_CHUNK)

    def chunk(ci):
        q0 = ci * Q_CHUNK
        own = q0 // MOBA_BLOCK
        qc = lax.dynamic_slice_in_dim(q, q0, Q_CHUNK, axis=2)
        g = jnp.einsum('bhcd,bhnd->bhcn', qc.astype(jnp.float32), k_mean)
        g = jnp.where(block_ids < own, g, -jnp.inf)
        _, idx = lax.top_k(g, topk)
        valid = idx < own
        k_sel = kb[bi, hi, idx]
        v_sel = vb[bi, hi, idx]
        s_sel = jnp.einsum('bhcd,bhcjkd->bhcjk', qc, k_sel).astype(jnp.float32) * scale
        s_sel = jnp.where(valid[..., None], s_sel, NEG)
        k_own = lax.dynamic_index_in_dim(kb, own, axis=2, keepdims=False)
        v_own = lax.dynamic_index_in_dim(vb, own, axis=2, keepdims=False)
        s_own = jnp.einsum('bhcd,bhkd->bhck', qc, k_own).astype(jnp.float32) * scale
        qpos = q0 + q_off
        kpos = own * MOBA_BLOCK + key_off
        s_own = jnp.where(kpos[None, :] <= qpos[:, None], s_own, NEG)
        s = jnp.concatenate([s_sel.reshape(B, H, Q_CHUNK, topk * MOBA_BLOCK), s_own], axis=-1)
        p = jax.nn.softmax(s, axis=-1).astype(v.dtype)
        p_sel = p[..., :topk * MOBA_BLOCK].reshape(B, H, Q_CHUNK, topk, MOBA_BLOCK)
        p_own = p[..., topk * MOBA_BLOCK:]
        return (jnp.einsum('bhcjk,bhcjkd->bhcd', p_sel, v_sel)
                + jnp.einsum('bhck,bhkd->bhcd', p_own, v_own))

    out = lax.map(chunk, jnp.arange(n_chunks))
    out = jnp.transpose(out, (1, 2, 0, 3, 4)).reshape(B, H, S, HEAD_DIM)
    return out.transpose(0, 2, 1, 3).reshape(B, S, TOK_WIDTH)


def mem_cross_attention(qm, mem_n, w_mem_kv, q_gain, k_gain):
    B, S, _ = qm.shape
    M = mem_n.shape[1]
    k, v = jnp.split(mem_n @ w_mem_kv, 2, axis=-1)
    q = rms_norm(qm.reshape(B, S, MEM_HEADS, HEAD_DIM), q_gain)
    k = rms_norm(k.reshape(B, M, MEM_HEADS, HEAD_DIM), k_gain)
    v = v.reshape(B, M, MEM_HEADS, HEAD_DIM)
    s = jnp.einsum('bshd,bmhd->bhsm', q, k).astype(jnp.float32) * (HEAD_DIM ** -0.5)
    p = jax.nn.softmax(s, axis=-1).astype(v.dtype)
    o = jnp.einsum('bhsm,bmhd->bshd', p, v)
    return o.reshape(B, S, MEM_WIDTH)


def setup_inputs(seed: int = 0) -> dict:
    key = jax.random.key(seed)
    ks = jax.random.split(key, 20)
    f32 = jnp.float32

    def w(k, shape, fan_in):
        return jax.random.normal(k, shape, f32) * (fan_in ** -0.5)

    def gain(k, shape):
        return 1.0 + 0.1 * jax.random.normal(k, shape, f32)

    return {
        "x": jax.random.normal(ks[0], (BATCH, SEQ, D_MODEL), f32),
        "mem": jax.random.normal(ks[1], (BATCH, MEM_LEN, D_MODEL), f32),
        "g_mix": gain(ks[2], (DEPTH, D_MODEL)),
        "g_mem": gain(ks[3], (DEPTH, D_MODEL)),
        "g_mlp": gain(ks[4], (DEPTH, D_MODEL)),
        "w_in_pool": w(ks[5], (N_POOL_LAYERS, D_MODEL, TOK_WIDTH + MEM_WIDTH), D_MODEL),
        "w_pool_group": w(ks[6], (N_POOL_LAYERS, POOL_GROUPS, POOL_GROUP_WIDTH, POOL_GROUP_WIDTH), POOL_GROUP_WIDTH),
        "pool_scale": gain(ks[7], (N_POOL_LAYERS, TOK_WIDTH)),
        "w_in_moba": w(ks[8], (N_MOBA_LAYERS, D_MODEL, 3 * TOK_WIDTH + MEM_WIDTH), D_MODEL),
        "moba_q_gain": gain(ks[9], (N_MOBA_LAYERS, HEAD_DIM)),
        "moba_k_gain": gain(ks[10], (N_MOBA_LAYERS, HEAD_DIM)),
        "w_mem_kv": w(ks[11], (DEPTH, D_MODEL, 2 * MEM_WIDTH), D_MODEL),
        "mem_q_gain": gain(ks[12], (DEPTH, HEAD_DIM)),
        "mem_k_gain": gain(ks[13], (DEPTH, HEAD_DIM)),
        "w_out": w(ks[14], (DEPTH, TOK_WIDTH + MEM_WIDTH, D_MODEL), TOK_WIDTH + MEM_WIDTH),
        "w_ff1": w(ks[15], (DEPTH, D_MODEL, D_FF), D_MODEL),
        "w_ff2": w(ks[16], (DEPTH, D_FF, D_MODEL), D_FF),
    }


def reference(x, mem, g_mix, g_mem, g_mlp, w_in_pool, w_pool_group, pool_scale,
              w_in_moba, moba_q_gain, moba_k_gain, w_mem_kv, mem_q_gain, mem_k_gain,
              w_out, w_ff1, w_ff2):
    S = x.shape[1]
    cos, sin = rope_tables(S)
    for i in range(DEPTH):
        j = i // N_MIXERS
        u = rms_norm(x, g_mix[i])
        mem_n = rms_norm(mem, g_mem[i])
        if i % N_MIXERS == 0:
            h = u @ w_in_pool[j]
            tok = pool_mixer(h[..., :TOK_WIDTH], w_pool_group[j], pool_scale[j])
            qm = h[..., TOK_WIDTH:]
        else:
            h = u @ w_in_moba[j]
            tok = moba_mixer(h[..., :3 * TOK_WIDTH], moba_q_gain[j], moba_k_gain[j], cos, sin)
            qm = h[..., 3 * TOK_WIDTH:]
        mo = mem_cross_attention(qm, mem_n, w_mem_kv[i], mem_q_gain[i], mem_k_gain[i])
        x = x + jnp.concatenate([tok, mo], axis=-1) @ w_out[i]
        u = rms_norm(x, g_mlp[i])
        x = x + jnp.square(jax.nn.relu(u @ w_ff1[i])) @ w_ff2[i]
    return x
```

```python
import numpy as np
from contextlib import ExitStack
import concourse.bass as bass
import concourse.mybir as mybir
from concourse.bass_utils import run_bass_kernel_spmd

F32 = mybir.dt.float32
BF16 = mybir.dt.bfloat16
ALU = mybir.AluOpType
AF = mybir.ActivationFunctionType
AX = mybir.AxisListType

NCORES = 8
D = 1024
NB = 16
BS = 256
T = NB * BS
DFF = 4096
EPS = 1e-6
NEGB = -1.0e30
NS_DMA = 8


class Op:
    __slots__ = ("eng", "fn", "deps", "needs_inc", "sig", "is_dma", "is_async")

    def __init__(self, eng, fn, is_dma):
        self.is_async = False
        self.eng = eng
        self.fn = fn
        self.deps = []
        self.needs_inc = False
        self.sig = None
        self.is_dma = is_dma


ENGS = ("pe", "act", "dve", "pool", "sp")


class Sched:
    def __init__(self, nc, eng_sems, dma_sems):
        self.nc = nc
        self.eng_sem = eng_sems
        self.dma_sems = dma_sems
        self.cnt = {e: 0 for e in ENGS}
        self.dma_n = {e: 0 for e in dma_sems}
        self.slot_last = {e: [None] * NS_DMA for e in dma_sems}
        self.known = {e: {} for e in ENGS}
        self.last_writer = {}
        self.readers = {}
        self.ops = []
        self.n_total = 0

    def add(self, eng, fn, reads=(), writes=(), is_dma=False):
        op = Op(eng, fn, is_dma)
        xr = [r for r in reads if isinstance(r, tuple) and r[0] in ("bank", "psb")]
        if xr:
            reads = [r for r in reads if r not in xr]
            writes = list(writes) + [r for r in xr if r not in writes]
        deps = []
        for r in reads:
            lw = self.last_writer.get(r)
            if lw is not None:
                deps.append(lw)
        for w in writes:
            lw = self.last_writer.get(w)
            if lw is not None:
                deps.append(lw)
            for rd in self.readers.get(w, ()):
                deps.append(rd)
        seen = set()
        for d in deps:
            if d is op or id(d) in seen:
                continue
            seen.add(id(d))
            op.deps.append(d)
            d.needs_inc = True
        for r in reads:
            self.readers.setdefault(r, []).append(op)
        for w in writes:
            self.last_writer[w] = op
            self.readers[w] = []
        self.ops.append(op)
        return op

    def dma(self, q, out, in_, reads=(), writes=()):
        def fn(e, out=out, in_=in_):
            return e.dma_start(out=out, in_=in_)
        return self.add(q, fn, reads, writes, is_dma=True)

    def add_async(self, eng, fn, sem, val, reads=(), writes=()):
        op = self.add(eng, fn, reads, writes)
        op.is_async = True
        op.sig = (sem, val)
        return op

    def flush(self):
        ops = self.ops
        tails = []
        for e in ENGS:
            last = None
            for o in ops:
                if o.eng == e and not o.is_dma and not o.is_async and o.fn is not None:
                    last = o
            if last is not None:
                tails.append(last)
        tails += [o for o in ops if o.is_dma or o.is_async]
        for t_ in tails:
            t_.needs_inc = True
        for e in ENGS:
            b = Op(e, None, False)
            b.deps = list(tails)
            ops.append(b)
        for o in ops:
            if o.is_dma:
                q = o.eng
                n = self.dma_n[q]
                slot = n % NS_DMA
                prev = self.slot_last[q][slot]
                if prev is not None:
                    o.deps.append(prev)
                o.sig = (self.dma_sems[q][slot], 16 * (n // NS_DMA + 1))
                self.slot_last[q][slot] = o
                self.dma_n[q] = n + 1
            elif o.is_async:
                pass
            elif o.needs_inc and o.fn is not None:
                self.cnt[o.eng] += 1
                o.sig = (self.eng_sem[o.eng], self.cnt[o.eng])
        nc = self.nc
        with nc.Block() as blk:
            for e in ENGS:
                eops = [o for o in ops if o.eng == e]
                if not eops:
                    continue
                deco = {"pe": blk.tensor, "act": blk.scalar, "dve": blk.vector,
                        "pool": blk.gpsimd, "sp": blk.sync}[e]

                def body(h, eops=eops, e=e):
                    known = self.known[e]
                    for o in eops:
                        for d in o.deps:
                            if d.sig is None:
                                continue
                            sem, val = d.sig
                            k = id(sem)
                            if known.get(k, 0) < val:
                                h.wait_ge(sem, val)
                                known[k] = val
                        if o.fn is None:
                            continue
                        ins = o.fn(h)
                        if o.is_dma:
                            ins.then_inc(o.sig[0], 16)
                        elif o.is_async:
                            ins.then_inc(o.sig[0], 1)
                        elif o.sig is not None:
                            ins.then_inc(o.sig[0], 1)
                deco(body)
        self.n_total += len(ops)
        self.ops = []
        self.last_writer = {}
        self.readers = {}


class _Stop(Exception):
    pass


def build_program(n_layers=4, stop_at=None):
    nc = bass.Bass("TRN2", target_bir_lowering=False)
    es = ExitStack()

    stopped = [False]

    def chk(tag):
        if stop_at == tag:
            S.flush()
            stopped[0] = True
        return stopped[0]

    def din(name, shape, dt=F32):
        return nc.dram_tensor(name, list(shape), dt, kind="ExternalInput").ap()

    def dscr(name, shape, dt):
        return nc.dram_tensor(name, list(shape), dt).ap()

    xT_in = din("xT", [D, T])
    xhT_in = din("xhT", [D, NB * 16])
    memT_in = din("memT", [D, 256])
    w_in_pool = din("w_in_pool", [2, D, 1024])
    w_pool_group = din("w_pool_group", [2, 4, 192, 192])
    w_in_moba = din("w_in_moba", [2, D, 2560])
    w_mem_kv = din("w_mem_kv", [4, D, 512])
    w_out = din("w_out", [4, D, D])
    w_ff1 = din("w_ff1", [4, D, DFF])
    w_ff2 = din("w_ff2", [4, DFF, D])
    gvec_in = din("gvec", [128, 3, 4, 8])
    pscale_in = din("pscale", [128, 2, 8])
    memgain_in = din("memgain", [128, 4, 2, 64])
    mobagain_in = din("mobagain", [128, 2, 24, 64])
    ident_in = din("ident", [128, 128])
    mmix_in = din("mmix", [128, 5, 4, 256])
    cos_in = din("cosT", [128, 32, 32])
    sin_in = din("sinT", [128, 32, 32])
    candb_in = din("candb", [128, NB, 64])
    cmask_in = din("cmask", [128, 2, 2, 256])
    out_T = nc.dram_tensor("outT", [D, T], F32, kind="ExternalOutput").ap()

    xTs = dscr("xTs", [D, T], F32)
    xb2 = dscr("xb2", [D, T], BF16)
    cat = dscr("cat", [10 * 128, T], BF16)
    QTd = dscr("QTd", [768, T], BF16)
    KTsrc = dscr("KTsrc", [8 * 768, 512], BF16)
    Vsrc = dscr("Vsrc", [8 * 128, 6 * 4 * 130], BF16)
    KMsrc = dscr("KMsrc", [768, NB], BF16)
    KTall = dscr("KTall", [8 * 4 * 768, 512], BF16)
    Vall = dscr("Vall", [8 * 4 * 128, 4 * 6 * 130], BF16)
    KMall = dscr("KMall", [4 * 768, NB], BF16)
    HTsrc = dscr("HTsrc", [512, 768], BF16)
    HTall = dscr("HTall", [4 * 512, 768], BF16)

    eng_sems = {e: es.enter_context(nc.semaphore("s_" + e)) for e in ENGS}
    dma_sems = {q: [es.enter_context(nc.semaphore("d_%s%d" % (q, i))) for i in range(NS_DMA)]
                for q in ("sp", "pool", "act")}
    cc_sem = es.enter_context(nc.semaphore("ccsem"))
    S = Sched(nc, eng_sems, dma_sems)

    psq = [es.enter_context(nc.psum_tensor("psq_%d" % i, [128, 1024], F32)) for i in range(4)]

    def bank(k):
        return psq[k // 2][:, (k % 2) * 512:(k % 2) * 512 + 512]

    def dbl(i):
        return psq[i][:, :]

    ps2 = [dbl(0), dbl(1), dbl(3)]
    psb = [bank(6).bitcast(BF16), bank(7).bitcast(BF16)]

    def BK(k):
        return ("bank", k)

    sbn = [0]

    def sb(name, shape, dt, stack=es):
        sbn[0] += 1
        return stack.enter_context(nc.sbuf_tensor("sb%d_%s" % (sbn[0], name), list(shape), dt))

    ident = sb("ident", [128, 128], BF16)
    ones_bf = sb("ones_bf", [128, 128], BF16)
    ccdummy = sb("ccdummy", [128, 8], F32)
    gvec = sb("gvec", [128, 3, 4, 8], F32)
    pscale = sb("pscale", [128, 2, 8], F32)
    memgain = sb("memgain", [128, 4, 2, 64], F32)

    S.dma("pool", ident[:], ident_in[:, :], writes=["ident"])
    S.add("dve", lambda e: e.memset(ones_bf[:], 1.0), writes=["ones"])
    S.dma("sp", gvec[:], gvec_in[:, :, :, :], writes=["gvec"])
    S.dma("sp", pscale[:], pscale_in[:, :, :], writes=["pscale"])
    S.dma("sp", memgain[:], memgain_in[:, :, :, :], writes=["memgain"])
    S.flush()

    def mm(out_ap, pieces, reads, writes):
        return mmg([(out_ap, pieces)], reads, writes)

    def mmg(groups, reads, writes):
        def fn(pe, groups=groups):
            ins = None
            for (out_ap, pieces) in groups:
                n = len(pieces)
                for i, (l, r) in enumerate(pieces):
                    ins = pe.matmul(out_ap, lhsT=l, rhs=r, start=(i == 0), stop=(i == n - 1))
            return ins
        return S.add("pe", fn, reads, writes)

    def fm_rstd(src, W, sq, rs, rstd, bk, tag, src_res):
        fm_sq(src, W, sq, tag, src_res)
        fm_rstd2(W, sq, rs, rstd, bk, tag)

    def fm_sq(src, W, sq, tag, src_res):
        S.add("act", lambda e: e.activation(out=sq[:, :, 0:W], in_=src[:, :, 0:W], func=AF.Square),
              reads=src_res, writes=[tag + "sq"])

    def fm_rstd2(W, sq, rs, rstd, bk, tag):
        mm(bank(bk)[:, 0:W], [(ones_bf[:, :], sq[:, c, 0:W]) for c in range(8)],
           reads=[tag + "sq", "ones"], writes=[BK(bk)])
        S.add("act", lambda e: e.activation(out=rs[:, 0:W], in_=bank(bk)[:, 0:W], func=AF.Ln,
                                            scale=1.0 / D, bias=EPS),
              reads=[BK(bk)], writes=[tag + "rs"])
        S.add("act", lambda e: e.activation(out=rstd[:, 0:W], in_=rs[:, 0:W], func=AF.Exp, scale=-0.5),
              reads=[tag + "rs"], writes=[tag + "rstd"])

    def fm_apply(src, W, g_ap, rstd, dst, tag, src_res, dst_res):
        for c in range(8):
            S.add("dve", lambda e, c=c: e.scalar_tensor_tensor(
                out=dst[:, c, 0:W], in0=src[:, c, 0:W], scalar=g_ap[:, c:c + 1], in1=rstd[:, 0:W],
                op0=ALU.mult, op1=ALU.mult),
                reads=list(src_res) + [tag + "rstd", "gvec"], writes=[dst_res])

    def mem_prep(li, st, kT, vP):
        memT = sb("memT", [128, 8, 256], F32, st)
        msq = sb("msq", [128, 8, 256], BF16, st)
        mrs = sb("mrs", [128, 256], F32, st)
        mrstd = sb("mrstd", [128, 256], F32, st)
        memb = sb("memb", [128, 8, 256], BF16, st)
        Wkv = sb("Wkv", [128, 8, 512], BF16, st)
        kf = sb("kf", [128, 2, 256], F32, st)
        ksq = sb("ksq", [128, 2, 256], F32, st)
        kss = sb("kss", [128, 2, 4], F32, st)
        krs = sb("krs", [128, 2, 4], F32, st)
        kn = sb("kn", [128, 2, 256], F32, st)
        knb = sb("knb", [128, 2, 256], BF16, st)
        S.dma("sp", memT[:], memT_in.rearrange("(c p) m -> p c m", p=128), writes=["memT"])
        S.dma("pool", Wkv[:], w_mem_kv[li].rearrange("(c p) n -> p c n", p=128), writes=["Wkv"])
        S.add("pool", lambda e: e.memset(vP[:], 1.0), writes=["vP"])
        fm_rstd(memT, 256, msq, mrs, mrstd, 4, "m", ["memT"])
        fm_apply(memT, 256, gvec[:, 1, li, :], mrstd, memb, "m", ["memT"], "memb")
        for mt in range(2):
            bk = mt
            mm(bank(bk), [(memb[:, c, mt * 128:(mt + 1) * 128], Wkv[:, c, :]) for c in range(8)],
               reads=["memb", "Wkv"], writes=[BK(bk)])
            S.add("act", lambda e, mt=mt, bk=bk: e.activation(out=kf[:, mt, :], in_=bank(bk)[:, 0:256],
                                                              func=AF.Copy),
                  reads=[BK(bk)], writes=[("kf", mt)])
            S.add("act", lambda e, mt=mt, bk=bk: e.activation(
                out=vP[:, mt, :, 0:64], in_=bank(bk)[:, 256:512].rearrange("p (h d) -> p h d", h=4),
                func=AF.Copy), reads=[BK(bk)], writes=["vP"])
            S.add("dve", lambda e, mt=mt: e.tensor_tensor(out=ksq[:, mt, :], in0=kf[:, mt, :],
                                                          in1=kf[:, mt, :], op=ALU.mult),
                  reads=[("kf", mt)], writes=[("ksq", mt)])
            S.add("dve", lambda e, mt=mt: e.tensor_reduce(
                out=kss[:, mt, :], in_=ksq[:, mt, :].rearrange("p (h d) -> p h d", h=4),
                axis=AX.X, op=ALU.add), reads=[("ksq", mt)], writes=[("kss", mt)])
            S.add("act", lambda e, mt=mt: e.activation(out=krs[:, mt, :], in_=kss[:, mt, :], func=AF.Ln,
                                                       scale=1.0 / 64, bias=EPS),
                  reads=[("kss", mt)], writes=[("krs", mt)])
            S.add("act", lambda e, mt=mt: e.activation(out=kss[:, mt, :], in_=krs[:, mt, :], func=AF.Exp, scale=-0.5),
                  reads=[("krs", mt)], writes=[("kss", mt)])
            S.add("dve", lambda e, mt=mt: e.tensor_tensor(
                out=kn[:, mt, :].rearrange("p (h d) -> p h d", h=4),
                in0=kf[:, mt, :].rearrange("p (h d) -> p h d", h=4),
                in1=kss[:, mt, :].unsqueeze(2).to_broadcast([128, 4, 64]), op=ALU.mult),
                reads=[("kf", mt), ("kss", mt)], writes=[("kn", mt)])
            S.add("dve", lambda e, mt=mt: e.tensor_tensor(
                out=knb[:, mt, :].rearrange("p (h d) -> p h d", h=4),
                in0=kn[:, mt, :].rearrange("p (h d) -> p h d", h=4),
                in1=memgain[:, li, 1, :].unsqueeze(1).to_broadcast([128, 4, 64]), op=ALU.mult),
                reads=[("kn", mt), "memgain"], writes=[("knb", mt)])
            def tpk(e, mt=mt):
                for pr in range(2):
                    ins = e.transpose(out=psb[0][:, pr * 128:(pr + 1) * 128], in_=knb[:, mt, pr * 128:(pr + 1) * 128],
                                      identity=ident[:, :])
                return ins
            S.add("pe", tpk, reads=[("knb", mt), "ident"], writes=[("psb", 0)])
            S.add("dve", lambda e, mt=mt: e.tensor_copy(
                out=kT[:, :, mt * 128:(mt + 1) * 128], in_=psb[0][:, 0:256].rearrange("p (a b) -> p a b", a=2)),
                reads=[("psb", 0)], writes=["kT"])

    def mem_attn(li, s, par, tl, kT, vP, qm_res, mo_piece0, part="all"):
        qm, qsq, qss, qrs, qn, qnb, qT, PT, onum, orc, ob, moT = tl
        for tt in (range(2) if part in ("all", "prep") else ()):
            S.add("pool", lambda e, tt=tt: e.tensor_tensor(out=qsq[:, tt, :], in0=qm[:, tt, :],
                                                           in1=qm[:, tt, :], op=ALU.mult),
                  reads=[qm_res(tt)], writes=[("qsq", par, tt)])
            S.add("dve", lambda e, tt=tt: e.tensor_reduce(
                out=qss[:, tt, :], in_=qsq[:, tt, :].rearrange("p (h d) -> p h d", h=4),
                axis=AX.X, op=ALU.add), reads=[("qsq", par, tt)], writes=[("qss", par, tt)])
            S.add("act", lambda e, tt=tt: e.activation(out=qrs[:, tt, :], in_=qss[:, tt, :], func=AF.Ln,
                                                       scale=1.0 / 64, bias=EPS),
                  reads=[("qss", par, tt)], writes=[("qrs", par, tt)])
            S.add("act", lambda e, tt=tt: e.activation(out=qss[:, tt, :], in_=qrs[:, tt, :], func=AF.Exp, scale=-0.5),
                  reads=[("qrs", par, tt)], writes=[("qss", par, tt)])
            S.add("dve", lambda e, tt=tt: e.tensor_tensor(
                out=qn[:, tt, :].rearrange("p (h d) -> p h d", h=4),
                in0=qm[:, tt, :].rearrange("p (h d) -> p h d", h=4),
                in1=qss[:, tt, :].unsqueeze(2).to_broadcast([128, 4, 64]), op=ALU.mult),
                reads=[qm_res(tt), ("qss", par, tt)], writes=[("qn", par, tt)])
            S.add("pool", lambda e, tt=tt: e.tensor_tensor(
                out=qnb[:, tt, :].rearrange("p (h d) -> p h d", h=4),
                in0=qn[:, tt, :].rearrange("p (h d) -> p h d", h=4),
                in1=memgain[:, li, 0, :].unsqueeze(1).to_broadcast([128, 4, 64]), op=ALU.mult),
                reads=[("qn", par, tt), "memgain"], writes=[("qnb", par, tt)])

        if part == "prep":
            return

        def tpq(e):
            for tt in range(2):
                for pr in range(2):
                    ins = e.transpose(out=psb[1][:, (pr * 2 + tt) * 128:(pr * 2 + tt + 1) * 128],
                                      in_=qnb[:, tt, pr * 128:(pr + 1) * 128], identity=ident[:, :])
            return ins
        S.add("pe", tpq, reads=[("qnb", par, 0), ("qnb", par, 1), "ident"], writes=[("psb", 1)])
        S.add("act", lambda e: e.activation(out=qT[:, :, :], in_=psb[1][:, 0:512].rearrange("p (a b) -> p a b", a=2),
                                            func=AF.Copy),
              reads=[("psb", 1)], writes=[("qT", par, 0), ("qT", par, 1)])
        for h in range(4):
            pr, hh = h // 2, h % 2
            bk = 4 + hh
            rows = slice(hh * 64, hh * 64 + 64)
            mmg([(bank(bk)[:, half * 256:(half + 1) * 256],
                  [(kT[rows, pr, half * 128:(half + 1) * 128], qT[rows, pr, :])]) for half in range(2)],
                reads=["kT", ("qT", par, pr)], writes=[BK(bk)])
            S.add("act", lambda e, h=h, bk=bk: e.activation(
                out=PT[:, h, :], in_=bank(bk), func=AF.Exp, scale=0.125),
                reads=[BK(bk)], writes=[("PT", par, h)])
            for tt in range(2):
                mm(bank(2 + tt)[:, h * 65:(h + 1) * 65],
                   [(PT[:, h, half * 256 + tt * 128: half * 256 + (tt + 1) * 128], vP[:, half, h, :])
                    for half in range(2)],
                   reads=[("PT", par, h), "vP"], writes=[BK(2 + tt)])
        for tt in range(2):
            pv = bank(2 + tt)[:, 0:260].rearrange("p (h e) -> p h e", h=4)
            S.add("dve", lambda e, tt=tt, pv=pv: e.reciprocal(out=orc[:, tt, :], in_=pv[:, :, 64]),
                  reads=[BK(2 + tt)], writes=[("orc", par, tt)])
            S.add("dve", lambda e, tt=tt, pv=pv: e.tensor_tensor(
                out=ob[:, tt, :].rearrange("p (h d) -> p h d", h=4), in0=pv[:, :, 0:64],
                in1=orc[:, tt, :].unsqueeze(2).to_broadcast([128, 4, 64]), op=ALU.mult),
                reads=[BK(2 + tt), ("orc", par, tt)], writes=[("ob", par, tt)])

        def tpo(e):
            for tt in range(2):
                for pr in range(2):
                    ins = e.transpose(out=psb[0][:, (pr * 2 + tt) * 128:(pr * 2 + tt + 1) * 128],
                                      in_=ob[:, tt, pr * 128:(pr + 1) * 128], identity=ident[:, :])
            return ins
        S.add("pe", tpo, reads=[("ob", par, 0), ("ob", par, 1), "ident"], writes=[("psb", 0)])
        S.add("act", lambda e: e.activation(out=moT[:, :, :], in_=psb[0][:, 0:512].rearrange("p (a b) -> p a b", a=2),
                                            func=AF.Copy),
              reads=[("psb", 0)], writes=[("moT", par)])
        S.dma("pool", cat[mo_piece0 * 128:(mo_piece0 + 2) * 128, s * BS:(s + 1) * BS]
              .rearrange("(c p) t -> p c t", p=128), moT[:], reads=[("moT", par)], writes=[("cat_mo", s)])

    def mem_attn_tiles(st, par):
        sfx = "_%d" % par
        qm = sb("qm" + sfx, [128, 2, 256], F32, st)
        qsq = sb("qsq" + sfx, [128, 2, 256], F32, st)
        qss = sb("qss" + sfx, [128, 2, 4], F32, st)
        qrs = sb("qrs" + sfx, [128, 2, 4], F32, st)
        qn = sb("qn" + sfx, [128, 2, 256], F32, st)
        qnb = sb("qnb" + sfx, [128, 2, 256], BF16, st)
        qT = sb("qT" + sfx, [128, 2, 256], BF16, st)
        PT = sb("PT" + sfx, [128, 4, 512], BF16, st)
        onum = None
        orc = sb("orc" + sfx, [128, 2, 4], F32, st)
        ob = sb("ob" + sfx, [128, 2, 256], BF16, st)
        moT = sb("moT" + sfx, [128, 2, 256], BF16, st)
        return (qm, qsq, qss, qrs, qn, qnb, qT, PT, onum, orc, ob, moT)

    PIECES = [(g, sub) for g in range(4) for sub in range(2)]

    def phase_a_pool(li, x_src):
        j = li // 2
        first = (li == 0)
        W = 272 if first else 256
        with ExitStack() as st:
            kT = sb("kT", [128, 2, 256], BF16, st)
            vP = sb("vP", [128, 2, 4, 65], BF16, st)
            with ExitStack() as st2:
                mem_prep(li, st2, kT, vP)
                S.flush()
            if chk("memprep"):
                return
            Win = sb("Win", [128, 8, 1024], BF16, st)
            WgS = sb("WgS", [128, 8, 192], BF16, st)
            mmix = sb("mmix", [128, 5, 4, 256], BF16, st)
            hb = sb("hb", [128, NB, 2, 768], BF16, st)
            if first:
                hh = sb("hh", [16, NB, 768], BF16, st)
            else:
                cand = sb("cand", [128, NB, 768], BF16, st)
                zt = sb("zt", [16, 768], BF16, st)
            S.dma("pool", Win[:], w_in_pool[j].rearrange("(c p) n -> p c n", p=128), writes=["Win"])
            for g in range(4):
                S.dma("pool", WgS[:, 2 * g, :], w_pool_group[j, g, 0:128, :], writes=["WgS"])
                S.dma("pool", WgS[0:64, 2 * g + 1, :], w_pool_group[j, g, 128:192, :], writes=["WgS"])
            S.dma("pool", mmix[:], mmix_in[:, :, :, :], writes=["mmix"])
            with ExitStack() as st1:
                xt = [sb("xt%d" % i, [128, 8, W], F32, st1) for i in range(2)]
                sq = [sb("sq%d" % i, [128, 8, W], BF16, st1) for i in range(2)]
                rs = [sb("rs%d" % i, [128, W], F32, st1) for i in range(2)]
                rstd = [sb("rstd%d" % i, [128, W], F32, st1) for i in range(2)]
                xb = [sb("xb%d" % i, [128, 8, W], BF16, st1) for i in range(2)]
                mat = [mem_attn_tiles(st1, i) for i in range(2)]

                def loads(s):
                    p = s % 2
                    S.dma("sp", xt[p][:, :, 0:256], x_src[:, s * BS:(s + 1) * BS].rearrange("(c p) t -> p c t", p=128),
                          writes=[("xt", p)])
                    if first:
                        S.dma("sp", xt[p][:, :, 256:272],
                              xhT_in[:, s * 16:(s + 1) * 16].rearrange("(c p) t -> p c t", p=128),
                              writes=[("xth", p)])

                def stNa(s):
                    p = s % 2
                    xres = [("xt", p)] + ([("xth", p)] if first else [])
                    fm_sq(xt[p], W, sq[p], "a%d" % p, xres)

                def stNb(s):
                    p = s % 2
                    xres = [("xt", p)] + ([("xth", p)] if first else [])
                    fm_rstd2(W, sq[p], rs[p], rstd[p], 4, "a%d" % p)
                    fm_apply(xt[p], W, gvec[:, 0, li, :], rstd[p], xb[p], "a%d" % p, xres, ("xb", p))

                def stP(s):
                    p = s % 2
                    qm = mat[p][0]
                    for tt in range(2):
                        for cc in range(2):
                            bk = 2 * tt + cc
                            mm(bank(bk), [(xb[p][:, c, tt * 128:(tt + 1) * 128], Win[:, c, cc * 512:(cc + 1) * 512])
                                          for c in range(8)], reads=[("xb", p), "Win"], writes=[BK(bk)])
                        S.add("act", lambda e, tt=tt, s=s: e.activation(out=hb[:, s, tt, 0:512], in_=bank(2 * tt),
                                                                        func=AF.Copy),
                              reads=[BK(2 * tt)], writes=[("hb", s)])
                        S.add("act", lambda e, tt=tt, s=s: e.activation(out=hb[:, s, tt, 512:768],
                                                                        in_=bank(2 * tt + 1)[:, 0:256], func=AF.Copy),
                              reads=[BK(2 * tt + 1)], writes=[("hb", s)])
                        S.add("act", lambda e, tt=tt, qm=qm: e.activation(out=qm[:, tt, :],
                                                                          in_=bank(2 * tt + 1)[:, 256:512], func=AF.Copy),
                              reads=[BK(2 * tt + 1)], writes=[("qm", p, tt)])
                    if first:
                        mm(bank(5)[0:16, 0:512], [(xb[p][:, c, 256:272], Win[:, c, 0:512]) for c in range(8)],
                           reads=[("xb", p), "Win"], writes=[BK(5)])
                        S.add("act", lambda e, s=s: e.activation(out=hh[:, s, 0:512], in_=bank(5)[0:16, 0:512],
                                                                 func=AF.Copy),
                              reads=[BK(5)], writes=[("hh", s)])
                        mm(bank(5)[0:16, 0:256], [(xb[p][:, c, 256:272], Win[:, c, 512:768]) for c in range(8)],
                           reads=[("xb", p), "Win"], writes=[BK(5)])
                        S.add("act", lambda e, s=s: e.activation(out=hh[:, s, 512:768], in_=bank(5)[0:16, 0:256],
                                                                 func=AF.Copy),
                              reads=[BK(5)], writes=[("hh", s)])
                    else:
                        S.dma("pool", HTsrc[s * 16:(s + 1) * 16, :], hb[112:128, s, 1, :],
                              reads=[("hb", s)], writes=[("HTsrc", 0, s)])
                        if s + 1 < NB:
                            S.dma("pool", HTsrc[256 + (s + 1) * 16:256 + (s + 2) * 16, :], hb[112:128, s, 1, :],
                                  reads=[("hb", s)], writes=[("HTsrc", 1, s + 1)])

                def stM(s, part):
                    p = s % 2
                    mem_attn(li, s, p, mat[p], kT, vP, lambda tt, p=p: ("qm", p, tt), 8, part)

                loads(0)
                loads(1)
                stNa(0)
                stNb(0)
                stNa(1)
                for s in range(NB):
                    if s + 1 < NB:
                        stNb(s + 1)
                    if s + 2 < NB:
                        loads(s + 2)
                    stP(s)
                    if s + 2 < NB:
                        stNa(s + 2)
                    stM(s, "prep")
                    if s >= 1:
                        stM(s - 1, "rest")
                stM(NB - 1, "rest")
                if not first:
                    S.add("dve", lambda e: e.memset(zt[:], 0.0), writes=["zt"])
                    S.dma("pool", HTsrc[256:272, :], zt[:], reads=["zt"], writes=[("HTsrc", 1, 0)])
                S.flush()
            if not first:
                allgather(HTsrc, HTall)
                S.flush()
                for q in range(8):
                    S.dma("sp", cand[q * 16:(q + 1) * 16, :, :],
                          HTall[q * 256:(q + 1) * 256, :].rearrange("(s t) f -> t s f", t=16),
                          writes=["cand"])
            dT = sb("dT", [128, 8, 256], BF16, st)
            tokT = [sb("tokT%d" % i, [128, 8, 256], BF16, st) for i in range(2)]
            for s in range(NB):
                p = s % 2
                for pi, (g, sub) in enumerate(PIECES):
                    f0 = 192 * g + 128 * sub
                    M = 128 if sub == 0 else 64
                    bk = pi % 4
                    mA = mmix[:, 2, g, :] if s == 0 else mmix[:, 0, g, :]
                    pcs = [(hb[:, s, 0, f0:f0 + M], mA), (hb[:, s, 1, f0:f0 + M], mmix[:, 1, g, :])]
                    rd = [("hb", s), "mmix"]
                    if first:
                        pcs.append((hh[0:16, s, f0:f0 + M], mmix[0:16, 3, g, :]))
                        rd += [("hh", s)]
                    else:
                        pcs.append((cand[:, s, f0:f0 + M], mmix[:, 4, g, :]))
                        rd += ["cand"]
                    mm(bank(bk)[0:M, 0:256], pcs, reads=rd, writes=[BK(bk)])
                    S.add("dve", lambda e, pi=pi, M=M, bk=bk: e.tensor_copy(out=dT[0:M, pi, :], in_=bank(bk)[0:M, 0:256]),
                          reads=[BK(bk)], writes=[("dT", pi)])
                if s == 0 and chk("a_mix"):
                    return
                for pi, (g, osub) in enumerate(PIECES):
                    o0 = 128 * osub
                    Mo = 128 if osub == 0 else 64
                    bk = 4 + pi % 2
                    mm(bank(bk)[0:Mo, 0:256],
                       [(WgS[:, 2 * g, o0:o0 + Mo], dT[:, 2 * g, :]),
                        (WgS[0:64, 2 * g + 1, o0:o0 + Mo], dT[0:64, 2 * g + 1, :])],
                       reads=["WgS", ("dT", 2 * g), ("dT", 2 * g + 1)], writes=[BK(bk)])
                    S.add("act", lambda e, pi=pi, Mo=Mo, bk=bk, p=p: e.activation(
                        out=tokT[p][0:Mo, pi, :], in_=bank(bk)[0:Mo, 0:256], func=AF.Copy,
                        scale=pscale[0:Mo, j, pi:pi + 1]),
                        reads=[BK(bk), "pscale"], writes=[("tokT", p)])
                S.dma("pool", cat[0:8 * 128, s * BS:(s + 1) * BS].rearrange("(c p) t -> p c t", p=128), tokT[p][:],
                      reads=[("tokT", p)], writes=[("cat_tok", s)])
            S.flush()

    cc_sems = [es.enter_context(nc.semaphore("cca%d" % i)) for i in range(19)]
    cc_use = [0] * 19

    def allgather_async(k, src, dst, reads, writes):
        cc_use[k] += 1

        def fn(e):
            return e.collective_compute("AllGather", ALU.bypass, replica_groups=[[0, 1, 2, 3], [4, 5, 6, 7]],
                                        ins=[src.opt()], outs=[dst.opt()])
        S.add_async("pool", fn, cc_sems[k], cc_use[k], reads, writes)

    cc_count = [0]

    def allgather(src, dst):
        cc_count[0] += 1
        n = cc_count[0]

        def fn(e, n=n):
            e.collective_compute("AllGather", ALU.bypass, replica_groups=[[0, 1, 2, 3], [4, 5, 6, 7]],
                                 ins=[src.opt()], outs=[dst.opt()]).then_inc(cc_sem)
            e.wait_ge(cc_sem, n)
            return e.memset(ccdummy[:], 0.0)
        S.add("pool", fn, writes=["ccdummy"])


    def phase_a_moba(li, x_src):
        j = li // 2
        with ExitStack() as st:
            kT = sb("kT", [128, 2, 256], BF16, st)
            vP = sb("vP", [128, 2, 4, 65], BF16, st)
            with ExitStack() as st2:
                mem_prep(li, st2, kT, vP)
                S.flush()
            Win = sb("Win", [128, 8, 2560], BF16, st)
            for cc in range(5):
                S.dma("pool", Win[:, :, cc * 512:(cc + 1) * 512],
                      w_in_moba[j, :, cc * 512:(cc + 1) * 512].rearrange("(c p) n -> p c n", p=128), writes=[("Win", cc)])
            G24 = sb("G24", [128, 24, 64], F32, st)
            cosT = sb("cosT", [128, 32, 32], F32, st)
            sinT = sb("sinT", [128, 32, 32], F32, st)
            S.dma("sp", G24[:], mobagain_in[:, j, :, :], writes=["G24"])
            S.dma("sp", cosT[:], cos_in[:, :, :], writes=["rope"])
            S.dma("sp", sinT[:], sin_in[:, :, :], writes=["rope"])
            KMs = sb("KMs", [128, 6, NB], F32, st)
            KMb = sb("KMb", [128, 6, NB], BF16, st)
            xt = [sb("xt%d" % i, [128, 8, 256], F32, st) for i in range(2)]
            sq = [sb("sq%d" % i, [128, 8, 256], BF16, st) for i in range(2)]
            rs = [sb("rs%d" % i, [128, 256], F32, st) for i in range(2)]
            rstd = [sb("rstd%d" % i, [128, 256], F32, st) for i in range(2)]
            xb = [sb("xb%d" % i, [128, 8, 256], BF16, st) for i in range(2)]
            hf = [sb("hf%d" % i, [128, 2, 1536], F32, st) for i in range(2)]
            sq2 = sb("sq2", [128, 1536], F32, st)
            ss = sb("ss", [128, 24], F32, st)
            rsn = sb("rsn", [128, 24], F32, st)
            qn = sb("qn", [128, 1536], F32, st)
            t1 = sb("t1", [128, 24, 32], F32, st)
            t2 = sb("t2", [128, 24, 32], F32, st)
            t3 = sb("t3", [128, 24, 32], F32, st)
            t4 = sb("t4", [128, 24, 32], F32, st)
            qkb = [sb("qkb%d" % i, [128, 2, 1536], BF16, st) for i in range(2)]
            vb = [[sb("vb%d_%d" % (i, k), [128, 6, 2, 65], BF16, st) for k in range(2)] for i in range(2)]
            qTs = sb("qTs", [128, 6, 256], BF16, st)
            kTs = sb("kTs", [128, 6, 256], BF16, st)
            mat = [mem_attn_tiles(st, i) for i in range(2)]
            for i in range(2):
                for k in range(2):
                    S.add("pool", lambda e, i=i, k=k: e.memset(vb[i][k][:], 1.0), writes=[("vb", i, k)])

            def loads(s):
                p = s % 2
                S.dma("sp", xt[p][:], x_src[:, s * BS:(s + 1) * BS].rearrange("(c p) t -> p c t", p=128),
                      writes=[("xt", p)])

            def frontNa(s):
                p = s % 2
                fm_sq(xt[p], 256, sq[p], "a%d" % p, [("xt", p)])

            def frontNb(s):
                p = s % 2
                xres = [("xt", p)]
                fm_rstd2(256, sq[p], rs[p], rstd[p], 4, "a%d" % p)
                fm_apply(xt[p], 256, gvec[:, 0, li, :], rstd[p], xb[p], "a%d" % p, xres, ("xb", p))

            def frontP(s):
                p = s % 2
                qm = mat[p][0]
                for tt in range(2):
                    for cc in range(5):
                        bk = cc % 4
                        mm(bank(bk), [(xb[p][:, c, tt * 128:(tt + 1) * 128], Win[:, c, cc * 512:(cc + 1) * 512])
                                      for c in range(8)], reads=[("xb", p), ("Win", cc)], writes=[BK(bk)])
                        if cc < 3:
                            S.add("act", lambda e, tt=tt, cc=cc, bk=bk, p=p: e.activation(
                                out=hf[p][:, tt, cc * 512:(cc + 1) * 512], in_=bank(bk), func=AF.Copy),
                                reads=[BK(bk)], writes=[("hf", p, tt, cc)])
                        elif cc == 3:
                            S.add("act", lambda e, tt=tt, bk=bk, p=p: e.activation(
                                out=vb[p][tt][:, 0:4, :, 0:64], in_=bank(bk).rearrange("p (c h d) -> p c h d", c=4, h=2),
                                func=AF.Copy), reads=[BK(bk)], writes=[("vb", p, tt)])
                        else:
                            S.add("act", lambda e, tt=tt, bk=bk, p=p: e.activation(
                                out=vb[p][tt][:, 4:6, :, 0:64],
                                in_=bank(bk)[:, 0:256].rearrange("p (c h d) -> p c h d", c=2, h=2),
                                func=AF.Copy), reads=[BK(bk)], writes=[("vb", p, tt)])
                            S.add("act", lambda e, tt=tt, bk=bk, qm=qm: e.activation(
                                out=qm[:, tt, :], in_=bank(bk)[:, 256:512], func=AF.Copy),
                                reads=[BK(bk)], writes=[("qm", p, tt)])
                    ck, ltl = s // 2, (s % 2) * 2 + tt
                    S.dma("pool", Vsrc[ck * 128:(ck + 1) * 128, :].rearrange("p (c l e) -> p c l e", l=4, c=6)[:, :, ltl, :],
                          vb[p][tt][:].rearrange("p c h e -> p c (h e)"), reads=[("vb", p, tt)], writes=[("Vsrc", ck)])

            def rope(s, tts=(0, 1)):
                p = s % 2
                for tt in tts:
                    A3 = hf[p][:, tt, :].rearrange("p (h d) -> p h d", h=24)
                    hq = [("hf", p, tt, 0), ("hf", p, tt, 1), ("hf", p, tt, 2)]
                    S.add("act", lambda e, tt=tt, p=p: e.activation(out=sq2[:], in_=hf[p][:, tt, :], func=AF.Square),
                          reads=hq, writes=["sq2"])
                    S.add("dve", lambda e: e.tensor_reduce(out=ss[:], in_=sq2[:].rearrange("p (h d) -> p h d", h=24),
                                                           axis=AX.X, op=ALU.add), reads=["sq2"], writes=["ss"])
                    S.add("act", lambda e: e.activation(out=rsn[:], in_=ss[:], func=AF.Ln, scale=1.0 / 64, bias=EPS),
                          reads=["ss"], writes=["rsn"])
                    S.add("act", lambda e: e.activation(out=ss[:], in_=rsn[:], func=AF.Exp, scale=-0.5),
                          reads=["rsn"], writes=["ss"])
                    S.add("dve", lambda e, A3=A3: e.tensor_tensor(
                        out=qn[:].rearrange("p (h d) -> p h d", h=24), in0=A3,
                        in1=ss[:].unsqueeze(2).to_broadcast([128, 24, 64]), op=ALU.mult),
                        reads=hq + ["ss"], writes=["qn"])
                    S.add("pool", lambda e: e.tensor_tensor(out=qn[:], in0=qn[:],
                                                            in1=G24[:].rearrange("p h d -> p (h d)"), op=ALU.mult),
                          reads=["qn", "G24"], writes=["qn"])
                    q3 = qn[:].rearrange("p (h d) -> p h d", h=24)
                    x1, x2 = q3[:, :, 0:32], q3[:, :, 32:64]
                    ti = 2 * s + tt
                    cb = cosT[:, ti, :].unsqueeze(1).to_broadcast([128, 24, 32])
                    sn = sinT[:, ti, :].unsqueeze(1).to_broadcast([128, 24, 32])
                    o3 = qkb[p][:, tt, :].rearrange("p (h d) -> p h d", h=24)
                    S.add("dve", lambda e, x1=x1, cb=cb: e.tensor_tensor(out=t1[:], in0=x1, in1=cb, op=ALU.mult),
                          reads=["qn", "rope"], writes=["t1"])
                    S.add("pool", lambda e, x2=x2, sn=sn: e.tensor_tensor(out=t2[:], in0=x2, in1=sn, op=ALU.mult),
                          reads=["qn", "rope"], writes=["t2"])
                    S.add("dve", lambda e, o3=o3: e.tensor_tensor(out=o3[:, :, 0:32], in0=t1[:], in1=t2[:], op=ALU.subtract),
                          reads=["t1", "t2"], writes=[("qkb", p, tt, 0)])
                    S.add("pool", lambda e, x2=x2, cb=cb: e.tensor_tensor(out=t3[:], in0=x2, in1=cb, op=ALU.mult),
                          reads=["qn", "rope"], writes=["t3"])
                    S.add("dve", lambda e, x1=x1, sn=sn: e.tensor_tensor(out=t4[:], in0=x1, in1=sn, op=ALU.mult),
                          reads=["qn", "rope"], writes=["t4"])
                    S.add("pool", lambda e, o3=o3: e.tensor_tensor(out=o3[:, :, 32:64], in0=t3[:], in1=t4[:], op=ALU.add),
                          reads=["t3", "t4"], writes=[("qkb", p, tt, 1)])

            def back(s):
                p = s % 2
                qkres = [("qkb", p, tt, k) for tt in range(2) for k in range(2)]
                mmg([(bank(5)[:, hp:hp + 1], [(qkb[p][:, tt, 768 + hp * 128:768 + (hp + 1) * 128], ones_bf[:, 0:1])
                                              for tt in range(2)]) for hp in range(6)],
                    reads=qkres + ["ones"], writes=[BK(5)])
                S.add("act", lambda e, s=s: e.activation(out=KMs[:, :, s], in_=bank(5)[:, 0:6], func=AF.Copy, scale=1.0 / BS),
                      reads=[BK(5)], writes=["KMs"])
                for which, dst_t, dram in ((0, qTs, QTd), (1, kTs, KTsrc)):
                    def tpa(e, which=which, p=p):
                        for hp in range(4):
                            for tt in range(2):
                                ins = e.transpose(out=psb[0][:, (hp * 2 + tt) * 128:(hp * 2 + tt + 1) * 128],
                                                  in_=qkb[p][:, tt, which * 768 + hp * 128: which * 768 + (hp + 1) * 128],
                                                  identity=ident[:, :])
                        return ins

                    def tpb(e, which=which, p=p):
                        for hp in range(4, 6):
                            for tt in range(2):
                                ins = e.transpose(out=psb[1][:, ((hp - 4) * 2 + tt) * 128:((hp - 4) * 2 + tt + 1) * 128],
                                                  in_=qkb[p][:, tt, which * 768 + hp * 128: which * 768 + (hp + 1) * 128],
                                                  identity=ident[:, :])
                        return ins
                    S.add("pe", tpa, reads=qkres + ["ident"], writes=[("psb", 0)])
                    S.add("act", lambda e, dst_t=dst_t: e.activation(
                        out=dst_t[:, 0:4, :], in_=psb[0][:, 0:1024].rearrange("p (a b) -> p a b", a=4), func=AF.Copy),
                        reads=[("psb", 0)], writes=[("qkT", which)])
                    S.add("pe", tpb, reads=qkres + ["ident"], writes=[("psb", 1)])
                    S.add("dve", lambda e, dst_t=dst_t: e.tensor_copy(
                        out=dst_t[:, 4:6, :], in_=psb[1][:, 0:512].rearrange("p (a b) -> p a b", a=2)),
                        reads=[("psb", 1)], writes=[("qkT", which)])
                    if which == 0:
                        S.dma("pool", QTd[:, s * BS:(s + 1) * BS].rearrange("(c p) t -> p c t", p=128), dst_t[:],
                              reads=[("qkT", which)], writes=[("QTd", s)])
                    else:
                        ck = s // 2
                        S.dma("pool", KTsrc[ck * 768:(ck + 1) * 768, (s % 2) * BS:(s % 2 + 1) * BS]
                              .rearrange("(c p) t -> p c t", p=128), dst_t[:],
                              reads=[("qkT", which)], writes=[("KTsrc", ck)])
                mem_attn(li, s, p, mat[p], kT, vP, lambda tt, p=p: ("qm", p, tt), 6, "rest")
                if s % 2 == 1:
                    ck = s // 2
                    allgather_async(1 + 2 * ck, KTsrc[ck * 768:(ck + 1) * 768, :], KTall[ck * 3072:(ck + 1) * 3072, :],
                                    [("KTsrc", ck)], [("KTall", ck)])
                    allgather_async(2 + 2 * ck, Vsrc[ck * 128:(ck + 1) * 128, :], Vall[ck * 512:(ck + 1) * 512, :],
                                    [("Vsrc", ck)], [("Vall", ck)])

            loads(0)
            loads(1)
            frontNa(0)
            frontNb(0)
            frontNa(1)
            for i in range(NB + 2):
                if i + 1 < NB:
                    frontNb(i + 1)
                if i + 2 < NB:
                    loads(i + 2)
                if 1 <= i <= NB:
                    rope(i - 1, (0,))
                if i < NB:
                    frontP(i)
                if i + 2 < NB:
                    frontNa(i + 2)
                if 1 <= i <= NB:
                    rope(i - 1, (1,))
                if i >= 2:
                    back(i - 2)
                if i < NB:
                    mem_attn(li, i, i % 2, mat[i % 2], kT, vP, lambda tt, p=i % 2: ("qm", p, tt), 6, "prep")
            S.add("dve", lambda e: e.tensor_copy(out=KMb[:], in_=KMs[:]), reads=["KMs"], writes=["KMb"])
            S.dma("pool", KMsrc.rearrange("(c p) s -> p c s", p=128), KMb[:], reads=["KMb"], writes=["KMsrc"])
            allgather_async(0, KMsrc, KMall, ["KMsrc"], ["KMall"])
            S.flush()

    def phase_b(li):
        with ExitStack() as st:
            KMst = sb("KMst", [128, 6, 4, NB], BF16, st)
            KM = sb("KM", [128, 6, 64], BF16, st)
            candb = sb("candb", [128, NB, 64], F32, st)
            cmask = sb("cmask", [128, 2, 2, 256], BF16, st)
            for r in range(4):
                S.dma("sp", KMst[:, :, r, :], KMall[r * 768:(r + 1) * 768, :].rearrange("(c p) s -> p c s", p=128),
                      writes=["KMst"])
            S.add("dve", lambda e: e.tensor_copy(out=KM[:].rearrange("p c (s r) -> p c r s", r=4), in_=KMst[:]),
                  reads=["KMst"], writes=["KM"])
            S.dma("sp", candb[:], candb_in[:, :, :], writes=["candb"])
            S.dma("pool", cmask[:], cmask_in[:, :, :, :], writes=["cmask"])
            KTsb = [sb("KTsb%d" % i, [128, 64, 256], BF16, st) for i in range(2)]
            Vsb = [sb("Vsb%d" % i, [128, 128, 130], BF16, st) for i in range(2)]
            KTown = [sb("KTown%d" % i, [128, NB, 256], BF16, st) for i in range(2)]
            Vown = [sb("Vown%d" % i, [128, 32, 130], BF16, st) for i in range(2)]
            QT = [sb("QT%d" % i, [128, NB, 256], BF16, st) for i in range(2)]
            PT = [sb("PT%d" % i, [128, 1024], BF16, st) for i in range(3)]
            acc = [sb("acc%d" % i, [128, 4, 65], F32, st) for i in range(2)]
            tmp = [sb("tmp%d" % i, [128, 4, 65], F32, st) for i in range(2)]
            sel = [sb("sel%d" % i, [128, 4, 64], F32, st) for i in range(2)]
            gg = sb("gg", [128, 4, 64], F32, st)
            m8 = sb("m8", [128, 4, 8], F32, st)
            thr = sb("thr", [128, 4], F32, st)
            rc = sb("rc", [128, 4], F32, st)
            otok = sb("otok", [128, 2, 128], BF16, st)
            tokTt = [sb("tokTt%d" % i, [128, 256], BF16, st) for i in range(2)]

            def loads(hp):
                b = hp % 2
                ktv = KTsb[b][:].rearrange("p (c u r) k -> p c r u k", u=2, r=4)
                vsv = Vsb[b][:].rearrange("p (c u r h) e -> p c r u h e", u=2, r=4, h=2)
                for ck in range(8):
                    for r in range(4):
                        S.dma("sp", ktv[:, ck, r],
                              KTall[ck * 3072 + r * 768 + hp * 128: ck * 3072 + r * 768 + (hp + 1) * 128, :]
                              .rearrange("p (u k) -> p u k", u=2), writes=[("KTsb", b)])
                        S.dma("sp", vsv[:, ck, r],
                              Vall[ck * 512 + r * 128: ck * 512 + (r + 1) * 128, hp * 520:(hp + 1) * 520]
                              .rearrange("p (u h e) -> p u h e", u=2, h=2), writes=[("Vsb", b)])
                    S.dma("sp", KTown[b][:, 2 * ck:2 * ck + 2, :],
                          KTsrc[ck * 768 + hp * 128: ck * 768 + (hp + 1) * 128, :].rearrange("p (u k) -> p u k", u=2),
                          writes=[("KTown", b)])
                    S.dma("sp", Vown[b][:, 4 * ck:4 * ck + 4, :],
                          Vsrc[ck * 128:(ck + 1) * 128, hp * 520:(hp + 1) * 520].rearrange("p (l e) -> p l e", l=4),
                          writes=[("Vown", b)])
                S.dma("sp", QT[b][:], QTd[hp * 128:(hp + 1) * 128, :].rearrange("p (s k) -> p s k", k=256),
                      writes=[("QT", b)])

            def prologue(hp, s):
                b, sp_ = hp % 2, s % 2
                for hh in range(2):
                    rows = slice(hh * 64, hh * 64 + 64)
                    mmg([(bank(4 + hh)[:, qt * 64:(qt + 1) * 64],
                          [(QT[b][rows, s, qt * 128:(qt + 1) * 128], KM[rows, hp, :])]) for qt in range(2)],
                        reads=[("QT", b), "KM"], writes=[BK(4 + hh)])
                    S.add("dve", lambda e, hh=hh, s=s: e.tensor_tensor(
                        out=gg[:, 2 * hh:2 * hh + 2, :], in0=bank(4 + hh)[:, 0:128].rearrange("p (a b) -> p a b", a=2),
                        in1=candb[:, s, :].unsqueeze(1).to_broadcast([128, 2, 64]), op=ALU.add),
                        reads=[BK(4 + hh), "candb"], writes=[("gg", hh)])
                for i in range(4):
                    S.add("dve", lambda e, i=i: e.max(out=m8[:, i, :], in_=gg[:, i, :]),
                          reads=[("gg", i // 2)], writes=[("m8", i)])
                S.add("dve", lambda e: e.tensor_scalar_max(out=thr[:], in0=m8[:, :, 2], scalar1=-1.0e29),
                      reads=[("m8", i) for i in range(4)], writes=["thr"])
                S.add("dve", lambda e, sp_=sp_: e.tensor_tensor(
                    out=sel[sp_][:], in0=gg[:], in1=thr[:].unsqueeze(2).to_broadcast([128, 4, 64]), op=ALU.is_ge),
                    reads=[("gg", 0), ("gg", 1), "thr"], writes=[("sel", sp_)])
                S.add("pool", lambda e, sp_=sp_: e.memset(acc[sp_][:], 0.0), writes=[("acc", sp_)])

            SK = [(BK(0), BK(1)), (BK(2), BK(3)), (BK(6), BK(7))]

            def stage1(n, hp, s, jj, own):
                b, par = hp % 2, n % 3
                grp = []
                for hh in range(2):
                    rows = slice(hh * 64, hh * 64 + 64)
                    for half in range(2):
                        kt = KTown[b][rows, s, half * 128:(half + 1) * 128] if own else \
                            KTsb[b][rows, jj, half * 128:(half + 1) * 128]
                        grp.append((ps2[par][:, hh * 512 + half * 256: hh * 512 + (half + 1) * 256],
                                    [(kt, QT[b][rows, s, :])]))
                mmg(grp, reads=[("QT", b), ("KTown", b) if own else ("KTsb", b)], writes=list(SK[par]))
                S.add("act", lambda e, par=par: e.activation(out=PT[par][:], in_=ps2[par][:, :], func=AF.Exp, scale=0.125),
                      reads=list(SK[par]), writes=[("PT", par)])
                if own:
                    S.add("pool", lambda e, par=par: e.tensor_tensor(
                        out=PT[par][:], in0=PT[par][:], in1=cmask[:].rearrange("p a b c -> p (a b c)"), op=ALU.mult),
                        reads=[("PT", par), "cmask"], writes=[("PT", par)])

            def stage2(n, hp, s, jj, own):
                b, par3, par, sp_ = hp % 2, n % 3, n % 2, s % 2
                grp = []
                for hh in range(2):
                    for qt in range(2):
                        i = 2 * hh + qt
                        pcs = []
                        for half in range(2):
                            vv = Vown[b][:, 2 * s + half, hh * 65:(hh + 1) * 65] if own else \
                                Vsb[b][:, 2 * jj + half, hh * 65:(hh + 1) * 65]
                            c0 = hh * 512 + half * 256 + qt * 128
                            pcs.append((PT[par3][:, c0:c0 + 128], vv))
                        grp.append((bank(4 + par)[:, i * 65:(i + 1) * 65], pcs))
                mmg(grp, reads=[("PT", par3), ("Vown", b) if own else ("Vsb", b)], writes=[BK(4 + par)])
                pv = bank(4 + par)[:, 0:260].rearrange("p (a b) -> p a b", a=4)
                if own:
                    S.add("dve", lambda e, par=par, pv=pv: e.tensor_copy(out=tmp[par][:], in_=pv),
                          reads=[BK(4 + par)], writes=[("tmp", par)])
                else:
                    S.add("dve", lambda e, par=par, pv=pv, jj=jj, sp_=sp_: e.tensor_tensor(
                        out=tmp[par][:], in0=pv, in1=sel[sp_][:, :, jj:jj + 1].to_broadcast([128, 4, 65]), op=ALU.mult),
                        reads=[BK(4 + par), ("sel", sp_)], writes=[("tmp", par)])
                S.add("pool", lambda e, par=par, sp_=sp_: e.tensor_tensor(out=acc[sp_][:], in0=acc[sp_][:], in1=tmp[par][:],
                                                                         op=ALU.add),
                      reads=[("tmp", par), ("acc", sp_)], writes=[("acc", sp_)])

            def epilogue(hp, s):
                sp_ = s % 2
                S.add("dve", lambda e, sp_=sp_: e.reciprocal(out=rc[:], in_=acc[sp_][:, :, 64]),
                      reads=[("acc", sp_)], writes=["rc"])
                S.add("dve", lambda e, sp_=sp_: e.tensor_tensor(
                    out=otok[:].rearrange("p q (h d) -> p h q d", h=2),
                    in0=acc[sp_][:, :, 0:64].rearrange("p (h q) d -> p h q d", h=2),
                    in1=rc[:].rearrange("p (h q) -> p h q", h=2).unsqueeze(3).to_broadcast([128, 2, 2, 64]), op=ALU.mult),
                    reads=[("acc", sp_), "rc"], writes=["otok"])

                pbv = bank(4 + sp_).bitcast(BF16)

                def tpo(e, sp_=sp_, pbv=pbv):
                    for qt in range(2):
                        ins = e.transpose(out=pbv[:, qt * 128:(qt + 1) * 128], in_=otok[:, qt, :], identity=ident[:, :])
                    return ins
                S.add("pe", tpo, reads=["otok", "ident"], writes=[BK(4 + sp_)])
                S.add("act", lambda e, sp_=sp_, pbv=pbv: e.activation(out=tokTt[sp_][:], in_=pbv[:, 0:256], func=AF.Copy),
                      reads=[BK(4 + sp_)], writes=[("tokTt", sp_)])
                S.dma("pool", cat[hp * 128:(hp + 1) * 128, s * BS:(s + 1) * BS], tokTt[sp_][:],
                      reads=[("tokTt", sp_)], writes=[("cat", hp, s)])

            iters = [(hp, s, jj, jj == 4 * s + 4) for hp in range(6) for s in range(NB) for jj in range(4 * s + 5)]
            N = len(iters)
            loads(0)
            def back(n):
                hp, s, jj, own = iters[n]
                stage2(n, hp, s, jj, own)
                if own:
                    epilogue(hp, s)

            LOOK = 2
            pending = []
            for n in range(N):
                hp, s, jj, own = iters[n]
                if s == 0 and jj == LOOK + 1 and hp + 1 < 6:
                    loads(hp + 1)
                if jj == 0:
                    prologue(hp, s)
                stage1(n, hp, s, jj, own)
                pending.append(n)
                while len(pending) > LOOK:
                    back(pending.pop(0))
            while pending:
                back(pending.pop(0))
            S.flush()

    def phase_c(li, x_src, pool_layer, pre=None):
        TW = 512
        if pool_layer:
            kp = [(pi, 128 if sub == 0 else 64, 192 * g + 128 * sub) for pi, (g, sub) in enumerate(PIECES)]
            kp += [(8, 128, 768), (9, 128, 896)]
        else:
            kp = [(c, 128, 128 * c) for c in range(8)]
        NP = len(kp)
        with ExitStack() as st:
            Wo = sb("Wo", [128, NP, D], BF16, st)
            for (pi, K, r0) in kp:
                S.dma("pool", Wo[0:K, pi, :], w_out[li, r0:r0 + K, :], writes=["Wo"])
            if pre is not None:
                pre()
            catT = [sb("catT%d" % i, [128, NP, TW], BF16, st) for i in range(2)]
            xt = [sb("cxt%d" % i, [128, 8, TW], F32, st) for i in range(2)]
            sq = sb("csq", [128, 8, TW], BF16, st)
            rs = sb("crs", [128, TW], F32, st)
            rstd = sb("crstd", [128, TW], F32, st)
            xbt = [sb("cxb%d" % i, [128, 8, TW], BF16, st) for i in range(2)]

            def loads(t):
                p = t % 2
                S.dma("sp", catT[p][:], cat[0:NP * 128, t * TW:(t + 1) * TW].rearrange("(c p) t -> p c t", p=128),
                      writes=[("catT", p)])
                S.dma("sp", xt[p][:], x_src[:, t * TW:(t + 1) * TW].rearrange("(c p) t -> p c t", p=128),
                      writes=[("cxt", p, dc) for dc in range(8)])
            def outproj(t):
                p = t % 2
                for dc in range(8):
                    bk = dc % 4
                    mm(bank(bk), [(Wo[0:K, pi, dc * 128:(dc + 1) * 128], catT[p][0:K, pi, :]) for (pi, K, r0) in kp],
                       reads=["Wo", ("catT", p)], writes=[BK(bk)])
                    S.add("dve", lambda e, dc=dc, bk=bk, p=p: e.tensor_tensor(
                        out=xt[p][:, dc, :], in0=bank(bk), in1=xt[p][:, dc, :], op=ALU.add),
                        reads=[BK(bk), ("cxt", p, dc)], writes=[("cxt", p, dc)])
                S.dma("pool", xTs[:, t * TW:(t + 1) * TW].rearrange("(c p) t -> p c t", p=128), xt[p][:],
                      reads=[("cxt", p, dc) for dc in range(8)], writes=[("xTs", t)])

            def normst(t):
                p = t % 2
                xres = [("cxt", p, dc) for dc in range(8)]
                fm_rstd(xt[p], TW, sq, rs, rstd, 5, "c", xres)
                fm_apply(xt[p], TW, gvec[:, 2, li, :], rstd, xbt[p], "c", xres, ("cxb", p))
                S.dma("pool", xb2[:, t * TW:(t + 1) * TW].rearrange("(c p) t -> p c t", p=128), xbt[p][:],
                      reads=[("cxb", p)], writes=[("xb2", t)])

            NT = T // TW
            loads(0)
            outproj(0)
            for t in range(NT):
                if t + 1 < NT:
                    loads(t + 1)
                    outproj(t + 1)
                normst(t)
            S.flush()

    def ffn_w_load(li, hf, W1, W2):
        for c4 in range(4):
            S.dma("pool", W1[:, 2 * c4:2 * c4 + 2, :],
                  w_ff1[li, c4 * 256:(c4 + 1) * 256, hf * 2048:(hf + 1) * 2048].rearrange("(c p) n -> p c n", p=128),
                  writes=[("W1", hf, c4)])
        for c4 in range(4):
            S.dma("pool", W2[:, 4 * c4:4 * c4 + 4, :],
                  w_ff2[li, hf * 2048 + c4 * 512: hf * 2048 + (c4 + 1) * 512, :].rearrange("(f p) n -> p f n", p=128),
                  writes=[("W2", hf, c4)])

    def phase_d(li, last, WA):
        TW = 512
        NT = T // TW
        with ExitStack() as st:
            WB = (sb("W1b", [128, 8, 2048], BF16, st), sb("W2b", [128, 16, D], BF16, st))
            ffn_w_load(li, 1, *WB)
            xbt = [sb("dxb%d" % i, [128, 8, TW], BF16, st) for i in range(2)]
            xt = sb("dxt", [128, 8, TW], F32, st)
            hm = [sb("hm%d" % i, [128, 16, TW], BF16, st) for i in range(2)]
            rt = [sb("rt%d" % i, [128, TW], BF16, st) for i in range(3)]
            for hf in range(2):
                dst = out_T if (last and hf == 1) else xTs
                W1, W2 = WA if hf == 0 else WB
                w1res = [("W1", hf, c4) for c4 in range(4)]
                w2res = [("W2", hf, c4) for c4 in range(4)]

                def loads_b(t):
                    p = t % 2
                    S.dma("sp", xbt[p][:], xb2[:, t * TW:(t + 1) * TW].rearrange("(c p) t -> p c t", p=128),
                          writes=[("dxb", p)])

                def loads_x(t):
                    S.dma("sp", xt[:], xTs[:, t * TW:(t + 1) * TW].rearrange("(c p) t -> p c t", p=128),
                          writes=[("dxt", dc) for dc in range(8)])

                def ffn1(t):
                    p = t % 2
                    for fc in range(16):
                        bk = fc % 3
                        ri = fc % 3
                        mm(bank(bk), [(W1[:, c, fc * 128:(fc + 1) * 128], xbt[p][:, c, :]) for c in range(8)],
                           reads=w1res + [("dxb", p)], writes=[BK(bk)])
                        S.add("act", lambda e, bk=bk, ri=ri: e.activation(out=rt[ri][:], in_=bank(bk), func=AF.Relu),
                              reads=[BK(bk)], writes=[("rt", ri)])
                        eng = "dve" if fc % 2 == 0 else "pool"
                        S.add(eng, lambda e, fc=fc, ri=ri, p=p: e.tensor_tensor(
                            out=hm[p][:, fc, :], in0=rt[ri][:], in1=rt[ri][:], op=ALU.mult),
                            reads=[("rt", ri)], writes=[("hm", p, fc)])

                def ffn2(t):
                    p = t % 2
                    for dc in range(8):
                        bk = 3 + dc % 3
                        mm(bank(bk), [(W2[:, fc, dc * 128:(dc + 1) * 128], hm[p][:, fc, :]) for fc in range(16)],
                           reads=w2res + [("hm", p, fc) for fc in range(16)], writes=[BK(bk)])
                        S.add("dve", lambda e, dc=dc, bk=bk: e.tensor_tensor(
                            out=xt[:, dc, :], in0=bank(bk), in1=xt[:, dc, :], op=ALU.add),
                            reads=[BK(bk), ("dxt", dc)], writes=[("dxt", dc)])
                    S.dma("pool", dst[:, t * TW:(t + 1) * TW].rearrange("(c p) t -> p c t", p=128), xt[:],
                          reads=[("dxt", dc) for dc in range(8)], writes=[("xdst", t)])

                loads_b(0)
                ffn1(0)
                for t in range(NT):
                    if t + 1 < NT:
                        loads_b(t + 1)
                    loads_x(t)
                    if t + 1 < NT:
                        ffn1(t + 1)
                    ffn2(t)
                S.flush()

    for li in range(n_layers):
        x_src = xT_in if li == 0 else xTs
        if li % 2 == 0:
            phase_a_pool(li, x_src)
            if stopped[0] or chk("A%d" % li):
                break
        else:
            phase_a_moba(li, x_src)
            if chk("A%d" % li):
                break
            phase_b(li)
            if chk("B%d" % li):
                break
        with ExitStack() as stw:
            WA = (sb("W1a", [128, 8, 2048], BF16, stw), sb("W2a", [128, 16, D], BF16, stw))
            phase_c(li, x_src, li % 2 == 0, pre=lambda li=li, WA=WA: ffn_w_load(li, 0, *WA))
            phase_d(li, li == n_layers - 1, WA)
    es.close()
    return nc


POOL_WINDOWS = (2, 4, 8, 16)


def _mix_mats(r):
    m = np.zeros((128, 5, 4, 256), np.float32)
    for g, w in enumerate(POOL_WINDOWS):
        full = np.zeros((16 + 256, 256), np.float32)
        fullf = np.zeros((16 + 256, 256), np.float32)
        for t in range(256):
            for u in range(t - w + 1, t + 1):
                full[16 + u, t] += 1.0 / w
                if u >= 0:
                    fullf[16 + u, t] += 1.0 / min(t + 1, w)
            full[16 + t, t] -= 1.0
            fullf[16 + t, t] -= 1.0
        m[:, 0, g, :] = full[16:144]
        m[:, 1, g, :] = full[144:272]
        m[:, 2, g, :] = fullf[16:144] if r == 0 else full[16:144]
        m[0:16, 3, g, :] = full[0:16]
        rr = (r - 1) % 4
        v = 0 if r > 0 else 1
        q = rr * 2 + v
        m[q * 16:(q + 1) * 16, 4, g, :] = full[0:16]
    return m


def _host_prep(inputs):
    x = np.asarray(inputs["x"], np.float32)
    mem = np.asarray(inputs["mem"], np.float32)
    B, S_, _ = x.shape
    nblk = S_ // BS
    shared = {}
    for k in ("w_in_pool", "w_pool_group", "w_in_moba", "w_mem_kv", "w_out", "w_ff1", "w_ff2"):
        shared[k] = np.ascontiguousarray(np.asarray(inputs[k], np.float32))
    g3 = np.stack([np.asarray(inputs[k], np.float32) for k in ("g_mix", "g_mem", "g_mlp")])
    shared["gvec"] = np.ascontiguousarray(g3.reshape(3, 4, 8, 128).transpose(3, 0, 1, 2))
    ps = np.asarray(inputs["pool_scale"], np.float32)
    pscale = np.zeros((128, 2, 8), np.float32)
    for j in range(2):
        for g in range(4):
            pscale[:, j, 2 * g] = ps[j, 192 * g:192 * g + 128]
            pscale[0:64, j, 2 * g + 1] = ps[j, 192 * g + 128:192 * g + 192]
    shared["pscale"] = pscale
    mg = np.stack([np.asarray(inputs["mem_q_gain"], np.float32), np.asarray(inputs["mem_k_gain"], np.float32)], 1)
    shared["memgain"] = np.ascontiguousarray(np.broadcast_to(mg[None], (128, 4, 2, 64)))
    qg = np.asarray(inputs["moba_q_gain"], np.float32)
    kg = np.asarray(inputs["moba_k_gain"], np.float32)
    mb = np.concatenate([np.repeat(qg[:, None, :], 12, 1), np.repeat(kg[:, None, :], 12, 1)], 1)
    shared["mobagain"] = np.ascontiguousarray(np.broadcast_to(mb[None], (128, 2, 24, 64)))
    shared["ident"] = np.eye(128, dtype=np.float32)
    cm = np.zeros((128, 2, 2, 256), np.float32)
    kk = np.arange(128)
    for half in range(2):
        cm[:, :, half, :] = ((half * 128 + kk)[:, None] <= np.arange(256)[None, :])[:, None, :]
    shared["cmask"] = cm
    inv = 10000.0 ** (-np.arange(0, 64, 2, dtype=np.float32) / 64)
    in_maps = []
    for c in range(NCORES):
        b, r = c // 4, c % 4
        xb_ = x[b].reshape(nblk, BS, D)
        mine = xb_[r::4]
        m = dict(shared)
        m["xT"] = np.ascontiguousarray(mine.reshape(T, D).T)
        halo = np.zeros((NB, 16, D), np.float32)
        for s in range(NB):
            gb = 4 * s + r
            if gb > 0:
                halo[s] = xb_[gb - 1, BS - 16:]
        m["xhT"] = np.ascontiguousarray(halo.reshape(NB * 16, D).T)
        m["memT"] = np.ascontiguousarray(mem[b].T)
        m["mmix"] = _mix_mats(r)
        pos = (np.arange(T) // BS * 4 + r) * BS + np.arange(T) % BS
        ang = pos.astype(np.float32)[:, None] * inv[None, :]
        m["cosT"] = np.ascontiguousarray(np.cos(ang).astype(np.float32).reshape(32, 128, 32).transpose(1, 0, 2))
        m["sinT"] = np.ascontiguousarray(np.sin(ang).astype(np.float32).reshape(32, 128, 32).transpose(1, 0, 2))
        cb = np.zeros((NB, 64), np.float32)
        for s in range(NB):
            cb[s, 4 * s + r:] = NEGB
        m["candb"] = np.ascontiguousarray(np.broadcast_to(cb[None], (128, NB, 64)))
        in_maps.append(m)
    return in_maps


_PROG_CACHE = {}


def kernel(_n_layers=4, _cores=NCORES, **inputs):
    in_maps = _host_prep(inputs)[:_cores]
    if _n_layers not in _PROG_CACHE:
        _PROG_CACHE[_n_layers] = build_program(_n_layers)
    nc = _PROG_CACHE[_n_layers]
    res = run_bass_kernel_spmd(nc, in_maps, core_ids=list(range(_cores)))
    x = inputs["x"]
    B, S_, _ = x.shape
    nblk = S_ // BS
    out = np.full((B, nblk, BS, D), np.nan, np.float32)
    for c in range(_cores):
        b, r = c // 4, c % 4
        o = np.asarray(res.results[c]["outT"], np.float32)
        out[b, r::4] = o.T.reshape(NB, BS, D)
    return out.reshape(B, S_, D)
```

```python
import numpy as np
from contextlib import ExitStack
import concourse.bass as bass
import concourse.mybir as mybir
from concourse.bass_utils import run_bass_kernel_spmd

F32 = mybir.dt.float32
BF16 = mybir.dt.bfloat16
ALU = mybir.AluOpType
AF = mybir.ActivationFunctionType
AX = mybir.AxisListType

NCORES = 8
D = 1024
NB = 16
BS = 256
T = NB * BS
DFF = 4096
EPS = 1e-6
NEGB = -1.0e30
NS_DMA = 8


class Op:
    __slots__ = ("eng", "fn", "deps", "needs_inc", "sig", "is_dma", "is_async")

    def __init__(self, eng, fn, is_dma):
        self.is_async = False
        self.eng = eng
        self.fn = fn
        self.deps = []
        self.needs_inc = False
        self.sig = None
        self.is_dma = is_dma


ENGS = ("pe", "act", "dve", "pool", "sp")


class Sched:
    def __init__(self, nc, eng_sems, dma_sems):
        self.nc = nc
        self.eng_sem = eng_sems
        self.dma_sems = dma_sems
        self.cnt = {e: 0 for e in ENGS}
        self.dma_n = {e: 0 for e in dma_sems}
        self.slot_last = {e: [None] * NS_DMA for e in dma_sems}
        self.known = {e: {} for e in ENGS}
        self.last_writer = {}
        self.readers = {}
        self.ops = []
        self.n_total = 0

    def add(self, eng, fn, reads=(), writes=(), is_dma=False):
        op = Op(eng, fn, is_dma)
        xr = [r for r in reads if isinstance(r, tuple) and r[0] in ("bank", "psb")]
        if xr:
            reads = [r for r in reads if r not in xr]
            writes = list(writes) + [r for r in xr if r not in writes]
        deps = []
        for r in reads:
            lw = self.last_writer.get(r)
            if lw is not None:
                deps.append(lw)
        for w in writes:
            lw = self.last_writer.get(w)
            if lw is not None:
                deps.append(lw)
            for rd in self.readers.get(w, ()):
                deps.append(rd)
        seen = set()
        for d in deps:
            if d is op or id(d) in seen:
                continue
            seen.add(id(d))
            op.deps.append(d)
            d.needs_inc = True
        for r in reads:
            self.readers.setdefault(r, []).append(op)
        for w in writes:
            self.last_writer[w] = op
            self.readers[w] = []
        self.ops.append(op)
        return op

    def dma(self, q, out, in_, reads=(), writes=()):
        def fn(e, out=out, in_=in_):
            return e.dma_start(out=out, in_=in_)
        return self.add(q, fn, reads, writes, is_dma=True)

    def add_async(self, eng, fn, sem, val, reads=(), writes=()):
        op = self.add(eng, fn, reads, writes)
        op.is_async = True
        op.sig = (sem, val)
        return op

    def flush(self):
        ops = self.ops
        tails = []
        for e in ENGS:
            last = None
            for o in ops:
                if o.eng == e and not o.is_dma and not o.is_async and o.fn is not None:
                    last = o
            if last is not None:
                tails.append(last)
        tails += [o for o in ops if o.is_dma or o.is_async]
        for t_ in tails:
            t_.needs_inc = True
        for e in ENGS:
            b = Op(e, None, False)
            b.deps = list(tails)
            ops.append(b)
        for o in ops:
            if o.is_dma:
                q = o.eng
                n = self.dma_n[q]
                slot = n % NS_DMA
                prev = self.slot_last[q][slot]
                if prev is not None:
                    o.deps.append(prev)
                o.sig = (self.dma_sems[q][slot], 16 * (n // NS_DMA + 1))
                self.slot_last[q][slot] = o
                self.dma_n[q] = n + 1
            elif o.is_async:
                pass
            elif o.needs_inc and o.fn is not None:
                self.cnt[o.eng] += 1
                o.sig = (self.eng_sem[o.eng], self.cnt[o.eng])
        nc = self.nc
        with nc.Block() as blk:
            for e in ENGS:
                eops = [o for o in ops if o.eng == e]
                if not eops:
                    continue
                deco = {"pe": blk.tensor, "act": blk.scalar, "dve": blk.vector,
                        "pool": blk.gpsimd, "sp": blk.sync}[e]

                def body(h, eops=eops, e=e):
                    known = self.known[e]
                    for o in eops:
                        for d in o.deps:
                            if d.sig is None:
                                continue
                            sem, val = d.sig
                            k = id(sem)
                            if known.get(k, 0) < val:
                                h.wait_ge(sem, val)
                                known[k] = val
                        if o.fn is None:
                            continue
                        ins = o.fn(h)
                        if o.is_dma:
                            ins.then_inc(o.sig[0], 16)
                        elif o.is_async:
                            ins.then_inc(o.sig[0], 1)
                        elif o.sig is not None:
                            ins.then_inc(o.sig[0], 1)
                deco(body)
        self.n_total += len(ops)
        self.ops = []
        self.last_writer = {}
        self.readers = {}


class _Stop(Exception):
    pass


def build_program(n_layers=4, stop_at=None):
    nc = bass.Bass("TRN2", target_bir_lowering=False)
    es = ExitStack()

    stopped = [False]

    def chk(tag):
        if stop_at == tag:
            S.flush()
            stopped[0] = True
        return stopped[0]

    def din(name, shape, dt=F32):
        return nc.dram_tensor(name, list(shape), dt, kind="ExternalInput").ap()

    def dscr(name, shape, dt):
        return nc.dram_tensor(name, list(shape), dt).ap()

    xT_in = din("xT", [D, T])
    xhT_in = din("xhT", [D, NB * 16])
    memT_in = din("memT", [D, 256])
    w_in_pool = din("w_in_pool", [2, D, 1024])
    w_pool_group = din("w_pool_group", [2, 4, 192, 192])
    w_in_moba = din("w_in_moba", [2, D, 2560])
    w_mem_kv = din("w_mem_kv", [4, D, 512])
    w_out = din("w_out", [4, D, D])
    w_ff1 = din("w_ff1", [4, D, DFF])
    w_ff2 = din("w_ff2", [4, DFF, D])
    gvec_in = din("gvec", [128, 3, 4, 8])
    pscale_in = din("pscale", [128, 2, 8])
    memgain_in = din("memgain", [128, 4, 2, 64])
    mobagain_in = din("mobagain", [128, 2, 24, 64])
    ident_in = din("ident", [128, 128])
    mmix_in = din("mmix", [128, 5, 4, 256])
    cos_in = din("cosT", [128, 32, 32])
    sin_in = din("sinT", [128, 32, 32])
    candb_in = din("candb", [128, NB, 64])
    cmask_in = din("cmask", [128, 2, 2, 256])
    out_T = nc.dram_tensor("outT", [D, T], F32, kind="ExternalOutput").ap()

    xTs = dscr("xTs", [D, T], F32)
    xb2 = dscr("xb2", [D, T], BF16)
    cat = dscr("cat", [10 * 128, T], BF16)
    QTd = dscr("QTd", [768, T], BF16)
    KTsrc = dscr("KTsrc", [8 * 768, 512], BF16)
    Vsrc = dscr("Vsrc", [8 * 128, 6 * 4 * 130], BF16)
    KMsrc = dscr("KMsrc", [768, NB], BF16)
    KTall = dscr("KTall", [8 * 4 * 768, 512], BF16)
    Vall = dscr("Vall", [8 * 4 * 128, 4 * 6 * 130], BF16)
    KMall = dscr("KMall", [4 * 768, NB], BF16)
    HTsrc = dscr("HTsrc", [512, 768], BF16)
    HTall = dscr("HTall", [4 * 512, 768], BF16)

    eng_sems = {e: es.enter_context(nc.semaphore("s_" + e)) for e in ENGS}
    dma_sems = {q: [es.enter_context(nc.semaphore("d_%s%d" % (q, i))) for i in range(NS_DMA)]
                for q in ("sp", "pool", "act")}
    cc_sem = es.enter_context(nc.semaphore("ccsem"))
    S = Sched(nc, eng_sems, dma_sems)

    psq = [es.enter_context(nc.psum_tensor("psq_%d" % i, [128, 1024], F32)) for i in range(4)]

    def bank(k):
        return psq[k // 2][:, (k % 2) * 512:(k % 2) * 512 + 512]

    def dbl(i):
        return psq[i][:, :]

    ps2 = [dbl(0), dbl(1), dbl(3)]
    psb = [bank(6).bitcast(BF16), bank(7).bitcast(BF16)]

    def BK(k):
        return ("bank", k)

    sbn = [0]

    def sb(name, shape, dt, stack=es):
        sbn[0] += 1
        return stack.enter_context(nc.sbuf_tensor("sb%d_%s" % (sbn[0], name), list(shape), dt))

    ident = sb("ident", [128, 128], BF16)
    ones_bf = sb("ones_bf", [128, 128], BF16)
    ccdummy = sb("ccdummy", [128, 8], F32)
    gvec = sb("gvec", [128, 3, 4, 8], F32)
    pscale = sb("pscale", [128, 2, 8], F32)
    memgain = sb("memgain", [128, 4, 2, 64], F32)

    S.dma("pool", ident[:], ident_in[:, :], writes=["ident"])
    S.add("dve", lambda e: e.memset(ones_bf[:], 1.0), writes=["ones"])
    S.dma("sp", gvec[:], gvec_in[:, :, :, :], writes=["gvec"])
    S.dma("sp", pscale[:], pscale_in[:, :, :], writes=["pscale"])
    S.dma("sp", memgain[:], memgain_in[:, :, :, :], writes=["memgain"])
    S.flush()

    def mm(out_ap, pieces, reads, writes):
        return mmg([(out_ap, pieces)], reads, writes)

    def mmg(groups, reads, writes):
        def fn(pe, groups=groups):
            ins = None
            for (out_ap, pieces) in groups:
                n = len(pieces)
                for i, (l, r) in enumerate(pieces):
                    ins = pe.matmul(out_ap, lhsT=l, rhs=r, start=(i == 0), stop=(i == n - 1))
            return ins
        return S.add("pe", fn, reads, writes)

    def fm_rstd(src, W, sq, rs, rstd, bk, tag, src_res):
        fm_sq(src, W, sq, tag, src_res)
        fm_rstd2(W, sq, rs, rstd, bk, tag)

    def fm_sq(src, W, sq, tag, src_res):
        S.add("act", lambda e: e.activation(out=sq[:, :, 0:W], in_=src[:, :, 0:W], func=AF.Square),
              reads=src_res, writes=[tag + "sq"])

    def fm_rstd2(W, sq, rs, rstd, bk, tag):
        mm(bank(bk)[:, 0:W], [(ones_bf[:, :], sq[:, c, 0:W]) for c in range(8)],
           reads=[tag + "sq", "ones"], writes=[BK(bk)])
        S.add("act", lambda e: e.activation(out=rs[:, 0:W], in_=bank(bk)[:, 0:W], func=AF.Ln,
                                            scale=1.0 / D, bias=EPS),
              reads=[BK(bk)], writes=[tag + "rs"])
        S.add("act", lambda e: e.activation(out=rstd[:, 0:W], in_=rs[:, 0:W], func=AF.Exp, scale=-0.5),
              reads=[tag + "rs"], writes=[tag + "rstd"])

    def fm_apply(src, W, g_ap, rstd, dst, tag, src_res, dst_res):
        for c in range(8):
            S.add("dve", lambda e, c=c: e.scalar_tensor_tensor(
                out=dst[:, c, 0:W], in0=src[:, c, 0:W], scalar=g_ap[:, c:c + 1], in1=rstd[:, 0:W],
                op0=ALU.mult, op1=ALU.mult),
                reads=list(src_res) + [tag + "rstd", "gvec"], writes=[dst_res])

    def mem_prep(li, st, kT, vP):
        memT = sb("memT", [128, 8, 256], F32, st)
        msq = sb("msq", [128, 8, 256], BF16, st)
        mrs = sb("mrs", [128, 256], F32, st)
        mrstd = sb("mrstd", [128, 256], F32, st)
        memb = sb("memb", [128, 8, 256], BF16, st)
        Wkv = sb("Wkv", [128, 8, 512], BF16, st)
        kf = sb("kf", [128, 2, 256], F32, st)
        ksq = sb("ksq", [128, 2, 256], F32, st)
        kss = sb("kss", [128, 2, 4], F32, st)
        krs = sb("krs", [128, 2, 4], F32, st)
        kn = sb("kn", [128, 2, 256], F32, st)
        knb = sb("knb", [128, 2, 256], BF16, st)
        S.dma("sp", memT[:], memT_in.rearrange("(c p) m -> p c m", p=128), writes=["memT"])
        S.dma("pool", Wkv[:], w_mem_kv[li].rearrange("(c p) n -> p c n", p=128), writes=["Wkv"])
        S.add("pool", lambda e: e.memset(vP[:], 1.0), writes=["vP"])
        fm_rstd(memT, 256, msq, mrs, mrstd, 4, "m", ["memT"])
        fm_apply(memT, 256, gvec[:, 1, li, :], mrstd, memb, "m", ["memT"], "memb")
        for mt in range(2):
            bk = mt
            mm(bank(bk), [(memb[:, c, mt * 128:(mt + 1) * 128], Wkv[:, c, :]) for c in range(8)],
               reads=["memb", "Wkv"], writes=[BK(bk)])
            S.add("act", lambda e, mt=mt, bk=bk: e.activation(out=kf[:, mt, :], in_=bank(bk)[:, 0:256],
                                                              func=AF.Copy),
                  reads=[BK(bk)], writes=[("kf", mt)])
            S.add("act", lambda e, mt=mt, bk=bk: e.activation(
                out=vP[:, mt, :, 0:64], in_=bank(bk)[:, 256:512].rearrange("p (h d) -> p h d", h=4),
                func=AF.Copy), reads=[BK(bk)], writes=["vP"])
            S.add("dve", lambda e, mt=mt: e.tensor_tensor(out=ksq[:, mt, :], in0=kf[:, mt, :],
                                                          in1=kf[:, mt, :], op=ALU.mult),
                  reads=[("kf", mt)], writes=[("ksq", mt)])
            S.add("dve", lambda e, mt=mt: e.tensor_reduce(
                out=kss[:, mt, :], in_=ksq[:, mt, :].rearrange("p (h d) -> p h d", h=4),
                axis=AX.X, op=ALU.add), reads=[("ksq", mt)], writes=[("kss", mt)])
            S.add("act", lambda e, mt=mt: e.activation(out=krs[:, mt, :], in_=kss[:, mt, :], func=AF.Ln,
                                                       scale=1.0 / 64, bias=EPS),
                  reads=[("kss", mt)], writes=[("krs", mt)])
            S.add("act", lambda e, mt=mt: e.activation(out=kss[:, mt, :], in_=krs[:, mt, :], func=AF.Exp, scale=-0.5),
                  reads=[("krs", mt)], writes=[("kss", mt)])
            S.add("dve", lambda e, mt=mt: e.tensor_tensor(
                out=kn[:, mt, :].rearrange("p (h d) -> p h d", h=4),
                in0=kf[:, mt, :].rearrange("p (h d) -> p h d", h=4),
                in1=kss[:, mt, :].unsqueeze(2).to_broadcast([128, 4, 64]), op=ALU.mult),
                reads=[("kf", mt), ("kss", mt)], writes=[("kn", mt)])
            S.add("dve", lambda e, mt=mt: e.tensor_tensor(
                out=knb[:, mt, :].rearrange("p (h d) -> p h d", h=4),
                in0=kn[:, mt, :].rearrange("p (h d) -> p h d", h=4),
                in1=memgain[:, li, 1, :].unsqueeze(1).to_broadcast([128, 4, 64]), op=ALU.mult),
                reads=[("kn", mt), "memgain"], writes=[("knb", mt)])
            def tpk(e, mt=mt):
                for pr in range(2):
                    ins = e.transpose(out=psb[0][:, pr * 128:(pr + 1) * 128], in_=knb[:, mt, pr * 128:(pr + 1) * 128],
                                      identity=ident[:, :])
                return ins
            S.add("pe", tpk, reads=[("knb", mt), "ident"], writes=[("psb", 0)])
            S.add("dve", lambda e, mt=mt: e.tensor_copy(
                out=kT[:, :, mt * 128:(mt + 1) * 128], in_=psb[0][:, 0:256].rearrange("p (a b) -> p a b", a=2)),
                reads=[("psb", 0)], writes=["kT"])

    def mem_attn(li, s, par, tl, kT, vP, qm_res, mo_piece0, part="all"):
        qm, qsq, qss, qrs, qn, qnb, qT, PT, onum, orc, ob, moT = tl
        for tt in (range(2) if part in ("all", "prep") else ()):
            S.add("pool", lambda e, tt=tt: e.tensor_tensor(out=qsq[:, tt, :], in0=qm[:, tt, :],
                                                           in1=qm[:, tt, :], op=ALU.mult),
                  reads=[qm_res(tt)], writes=[("qsq", par, tt)])
            S.add("dve", lambda e, tt=tt: e.tensor_reduce(
                out=qss[:, tt, :], in_=qsq[:, tt, :].rearrange("p (h d) -> p h d", h=4),
                axis=AX.X, op=ALU.add), reads=[("qsq", par, tt)], writes=[("qss", par, tt)])
            S.add("act", lambda e, tt=tt: e.activation(out=qrs[:, tt, :], in_=qss[:, tt, :], func=AF.Ln,
                                                       scale=1.0 / 64, bias=EPS),
                  reads=[("qss", par, tt)], writes=[("qrs", par, tt)])
            S.add("act", lambda e, tt=tt: e.activation(out=qss[:, tt, :], in_=qrs[:, tt, :], func=AF.Exp, scale=-0.5),
                  reads=[("qrs", par, tt)], writes=[("qss", par, tt)])
            S.add("dve", lambda e, tt=tt: e.tensor_tensor(
                out=qn[:, tt, :].rearrange("p (h d) -> p h d", h=4),
                in0=qm[:, tt, :].rearrange("p (h d) -> p h d", h=4),
                in1=qss[:, tt, :].unsqueeze(2).to_broadcast([128, 4, 64]), op=ALU.mult),
                reads=[qm_res(tt), ("qss", par, tt)], writes=[("qn", par, tt)])
            S.add("pool", lambda e, tt=tt: e.tensor_tensor(
                out=qnb[:, tt, :].rearrange("p (h d) -> p h d", h=4),
                in0=qn[:, tt, :].rearrange("p (h d) -> p h d", h=4),
                in1=memgain[:, li, 0, :].unsqueeze(1).to_broadcast([128, 4, 64]), op=ALU.mult),
                reads=[("qn", par, tt), "memgain"], writes=[("qnb", par, tt)])

        if part == "prep":
            return

        def tpq(e):
            for tt in range(2):
                for pr in range(2):
                    ins = e.transpose(out=psb[1][:, (pr * 2 + tt) * 128:(pr * 2 + tt + 1) * 128],
                                      in_=qnb[:, tt, pr * 128:(pr + 1) * 128], identity=ident[:, :])
            return ins
        S.add("pe", tpq, reads=[("qnb", par, 0), ("qnb", par, 1), "ident"], writes=[("psb", 1)])
        S.add("act", lambda e: e.activation(out=qT[:, :, :], in_=psb[1][:, 0:512].rearrange("p (a b) -> p a b", a=2),
                                            func=AF.Copy),
              reads=[("psb", 1)], writes=[("qT", par, 0), ("qT", par, 1)])
        for h in range(4):
            pr, hh = h // 2, h % 2
            bk = 4 + hh
            rows = slice(hh * 64, hh * 64 + 64)
            mmg([(bank(bk)[:, half * 256:(half + 1) * 256],
                  [(kT[rows, pr, half * 128:(half + 1) * 128], qT[rows, pr, :])]) for half in range(2)],
                reads=["kT", ("qT", par, pr)], writes=[BK(bk)])
            S.add("act", lambda e, h=h, bk=bk: e.activation(
                out=PT[:, h, :], in_=bank(bk), func=AF.Exp, scale=0.125),
                reads=[BK(bk)], writes=[("PT", par, h)])
            for tt in range(2):
                mm(bank(2 + tt)[:, h * 65:(h + 1) * 65],
                   [(PT[:, h, half * 256 + tt * 128: half * 256 + (tt + 1) * 128], vP[:, half, h, :])
                    for half in range(2)],
                   reads=[("PT", par, h), "vP"], writes=[BK(2 + tt)])
        for tt in range(2):
            pv = bank(2 + tt)[:, 0:260].rearrange("p (h e) -> p h e", h=4)
            S.add("dve", lambda e, tt=tt, pv=pv: e.reciprocal(out=orc[:, tt, :], in_=pv[:, :, 64]),
                  reads=[BK(2 + tt)], writes=[("orc", par, tt)])
            S.add("dve", lambda e, tt=tt, pv=pv: e.tensor_tensor(
                out=ob[:, tt, :].rearrange("p (h d) -> p h d", h=4), in0=pv[:, :, 0:64],
                in1=orc[:, tt, :].unsqueeze(2).to_broadcast([128, 4, 64]), op=ALU.mult),
                reads=[BK(2 + tt), ("orc", par, tt)], writes=[("ob", par, tt)])

        def tpo(e):
            for tt in range(2):
                for pr in range(2):
                    ins = e.transpose(out=psb[0][:, (pr * 2 + tt) * 128:(pr * 2 + tt + 1) * 128],
                                      in_=ob[:, tt, pr * 128:(pr + 1) * 128], identity=ident[:, :])
            return ins
        S.add("pe", tpo, reads=[("ob", par, 0), ("ob", par, 1), "ident"], writes=[("psb", 0)])
        S.add("act", lambda e: e.activation(out=moT[:, :, :], in_=psb[0][:, 0:512].rearrange("p (a b) -> p a b", a=2),
                                            func=AF.Copy),
              reads=[("psb", 0)], writes=[("moT", par)])
        S.dma("pool", cat[mo_piece0 * 128:(mo_piece0 + 2) * 128, s * BS:(s + 1) * BS]
              .rearrange("(c p) t -> p c t", p=128), moT[:], reads=[("moT", par)], writes=[("cat_mo", s)])

    def mem_attn_tiles(st, par):
        sfx = "_%d" % par
        qm = sb("qm" + sfx, [128, 2, 256], F32, st)
        qsq = sb("qsq" + sfx, [128, 2, 256], F32, st)
        qss = sb("qss" + sfx, [128, 2, 4], F32, st)
        qrs = sb("qrs" + sfx, [128, 2, 4], F32, st)
        qn = sb("qn" + sfx, [128, 2, 256], F32, st)
        qnb = sb("qnb" + sfx, [128, 2, 256], BF16, st)
        qT = sb("qT" + sfx, [128, 2, 256], BF16, st)
        PT = sb("PT" + sfx, [128, 4, 512], BF16, st)
        onum = None
        orc = sb("orc" + sfx, [128, 2, 4], F32, st)
        ob = sb("ob" + sfx, [128, 2, 256], BF16, st)
        moT = sb("moT" + sfx, [128, 2, 256], BF16, st)
        return (qm, qsq, qss, qrs, qn, qnb, qT, PT, onum, orc, ob, moT)

    PIECES = [(g, sub) for g in range(4) for sub in range(2)]

    def phase_a_pool(li, x_src):
        j = li // 2
        first = (li == 0)
        W = 272 if first else 256
        with ExitStack() as st:
            kT = sb("kT", [128, 2, 256], BF16, st)
            vP = sb("vP", [128, 2, 4, 65], BF16, st)
            with ExitStack() as st2:
                mem_prep(li, st2, kT, vP)
                S.flush()
            if chk("memprep"):
                return
            Win = sb("Win", [128, 8, 1024], BF16, st)
            WgS = sb("WgS", [128, 8, 192], BF16, st)
            mmix = sb("mmix", [128, 5, 4, 256], BF16, st)
            hb = sb("hb", [128, NB, 2, 768], BF16, st)
            if first:
                hh = sb("hh", [16, NB, 768], BF16, st)
            else:
                cand = sb("cand", [128, NB, 768], BF16, st)
                zt = sb("zt", [16, 768], BF16, st)
            S.dma("pool", Win[:], w_in_pool[j].rearrange("(c p) n -> p c n", p=128), writes=["Win"])
            for g in range(4):
                S.dma("pool", WgS[:, 2 * g, :], w_pool_group[j, g, 0:128, :], writes=["WgS"])
                S.dma("pool", WgS[0:64, 2 * g + 1, :], w_pool_group[j, g, 128:192, :], writes=["WgS"])
            S.dma("pool", mmix[:], mmix_in[:, :, :, :], writes=["mmix"])
            with ExitStack() as st1:
                xt = [sb("xt%d" % i, [128, 8, W], F32, st1) for i in range(2)]
                sq = [sb("sq%d" % i, [128, 8, W], BF16, st1) for i in range(2)]
                rs = [sb("rs%d" % i, [128, W], F32, st1) for i in range(2)]
                rstd = [sb("rstd%d" % i, [128, W], F32, st1) for i in range(2)]
                xb = [sb("xb%d" % i, [128, 8, W], BF16, st1) for i in range(2)]
                mat = [mem_attn_tiles(st1, i) for i in range(2)]

                def loads(s):
                    p = s % 2
                    S.dma("sp", xt[p][:, :, 0:256], x_src[:, s * BS:(s + 1) * BS].rearrange("(c p) t -> p c t", p=128),
                          writes=[("xt", p)])
                    if first:
                        S.dma("sp", xt[p][:, :, 256:272],
                              xhT_in[:, s * 16:(s + 1) * 16].rearrange("(c p) t -> p c t", p=128),
                              writes=[("xth", p)])

                def stNa(s):
                    p = s % 2
                    xres = [("xt", p)] + ([("xth", p)] if first else [])
                    fm_sq(xt[p], W, sq[p], "a%d" % p, xres)

                def stNb(s):
                    p = s % 2
                    xres = [("xt", p)] + ([("xth", p)] if first else [])
                    fm_rstd2(W, sq[p], rs[p], rstd[p], 4, "a%d" % p)
                    fm_apply(xt[p], W, gvec[:, 0, li, :], rstd[p], xb[p], "a%d" % p, xres, ("xb", p))

                def stP(s):
                    p = s % 2
                    qm = mat[p][0]
                    for tt in range(2):
                        for cc in range(2):
                            bk = 2 * tt + cc
                            mm(bank(bk), [(xb[p][:, c, tt * 128:(tt + 1) * 128], Win[:, c, cc * 512:(cc + 1) * 512])
                                          for c in range(8)], reads=[("xb", p), "Win"], writes=[BK(bk)])
                        S.add("act", lambda e, tt=tt, s=s: e.activation(out=hb[:, s, tt, 0:512], in_=bank(2 * tt),
                                                                        func=AF.Copy),
                              reads=[BK(2 * tt)], writes=[("hb", s)])
                        S.add("act", lambda e, tt=tt, s=s: e.activation(out=hb[:, s, tt, 512:768],
                                                                        in_=bank(2 * tt + 1)[:, 0:256], func=AF.Copy),
                              reads=[BK(2 * tt + 1)], writes=[("hb", s)])
                        S.add("act", lambda e, tt=tt, qm=qm: e.activation(out=qm[:, tt, :],
                                                                          in_=bank(2 * tt + 1)[:, 256:512], func=AF.Copy),
                              reads=[BK(2 * tt + 1)], writes=[("qm", p, tt)])
                    if first:
                        mm(bank(5)[0:16, 0:512], [(xb[p][:, c, 256:272], Win[:, c, 0:512]) for c in range(8)],
                           reads=[("xb", p), "Win"], writes=[BK(5)])
                        S.add("act", lambda e, s=s: e.activation(out=hh[:, s, 0:512], in_=bank(5)[0:16, 0:512],
                                                                 func=AF.Copy),
                              reads=[BK(5)], writes=[("hh", s)])
                        mm(bank(5)[0:16, 0:256], [(xb[p][:, c, 256:272], Win[:, c, 512:768]) for c in range(8)],
                           reads=[("xb", p), "Win"], writes=[BK(5)])
                        S.add("act", lambda e, s=s: e.activation(out=hh[:, s, 512:768], in_=bank(5)[0:16, 0:256],
                                                                 func=AF.Copy),
                              reads=[BK(5)], writes=[("hh", s)])
                    else:
                        S.dma("pool", HTsrc[s * 16:(s + 1) * 16, :], hb[112:128, s, 1, :],
                              reads=[("hb", s)], writes=[("HTsrc", 0, s)])
                        if s + 1 < NB:
                            S.dma("pool", HTsrc[256 + (s + 1) * 16:256 + (s + 2) * 16, :], hb[112:128, s, 1, :],
                                  reads=[("hb", s)], writes=[("HTsrc", 1, s + 1)])

                def stM(s, part):
                    p = s % 2
                    mem_attn(li, s, p, mat[p], kT, vP, lambda tt, p=p: ("qm", p, tt), 8, part)

                loads(0)
                loads(1)
                stNa(0)
                stNb(0)
                stNa(1)
                for s in range(NB):
                    if s + 1 < NB:
                        stNb(s + 1)
                    if s + 2 < NB:
                        loads(s + 2)
                    stP(s)
                    if s + 2 < NB:
                        stNa(s + 2)
                    stM(s, "prep")
                    if s >= 1:
                        stM(s - 1, "rest")
                stM(NB - 1, "rest")
                if not first:
                    S.add("dve", lambda e: e.memset(zt[:], 0.0), writes=["zt"])
                    S.dma("pool", HTsrc[256:272, :], zt[:], reads=["zt"], writes=[("HTsrc", 1, 0)])
                S.flush()
            if not first:
                allgather(HTsrc, HTall)
                S.flush()
                for q in range(8):
                    S.dma("sp", cand[q * 16:(q + 1) * 16, :, :],
                          HTall[q * 256:(q + 1) * 256, :].rearrange("(s t) f -> t s f", t=16),
                          writes=["cand"])
            dT = sb("dT", [128, 8, 256], BF16, st)
            tokT = [sb("tokT%d" % i, [128, 8, 256], BF16, st) for i in range(2)]
            for i in range(2):
                S.add("dve", lambda e, i=i: e.memset(tokT[i][:], 0.0), writes=[("tokT", i)])
            for s in range(NB):
                p = s % 2
                for pi, (g, sub) in enumerate(PIECES):
                    f0 = 192 * g + 128 * sub
                    M = 128 if sub == 0 else 64
                    bk = pi % 4
                    mA = mmix[:, 2, g, :] if s == 0 else mmix[:, 0, g, :]
                    pcs = [(hb[:, s, 0, f0:f0 + M], mA), (hb[:, s, 1, f0:f0 + M], mmix[:, 1, g, :])]
                    rd = [("hb", s), "mmix"]
                    if first:
                        pcs.append((hh[0:16, s, f0:f0 + M], mmix[0:16, 3, g, :]))
                        rd += [("hh", s)]
                    else:
                        pcs.append((cand[:, s, f0:f0 + M], mmix[:, 4, g, :]))
                        rd += ["cand"]
                    mm(bank(bk)[0:M, 0:256], pcs, reads=rd, writes=[BK(bk)])
                    S.add("dve", lambda e, pi=pi, M=M, bk=bk: e.tensor_copy(out=dT[0:M, pi, :], in_=bank(bk)[0:M, 0:256]),
                          reads=[BK(bk)], writes=[("dT", pi)])
                if s == 0 and chk("a_mix"):
                    return
                for pi, (g, osub) in enumerate(PIECES):
                    o0 = 128 * osub
                    Mo = 128 if osub == 0 else 64
                    bk = 4 + pi % 2
                    mm(bank(bk)[0:Mo, 0:256],
                       [(WgS[:, 2 * g, o0:o0 + Mo], dT[:, 2 * g, :]),
                        (WgS[0:64, 2 * g + 1, o0:o0 + Mo], dT[0:64, 2 * g + 1, :])],
                       reads=["WgS", ("dT", 2 * g), ("dT", 2 * g + 1)], writes=[BK(bk)])
                    S.add("act", lambda e, pi=pi, Mo=Mo, bk=bk, p=p: e.activation(
                        out=tokT[p][0:Mo, pi, :], in_=bank(bk)[0:Mo, 0:256], func=AF.Copy,
                        scale=pscale[0:Mo, j, pi:pi + 1]),
                        reads=[BK(bk), "pscale"], writes=[("tokT", p)])
                S.dma("pool", cat[0:8 * 128, s * BS:(s + 1) * BS].rearrange("(c p) t -> p c t", p=128), tokT[p][:],
                      reads=[("tokT", p)], writes=[("cat_tok", s)])
            S.flush()

    cc_sems = [es.enter_context(nc.semaphore("cca%d" % i)) for i in range(19)]
    cc_use = [0] * 19

    def allgather_async(k, src, dst, reads, writes):
        cc_use[k] += 1

        def fn(e):
            return e.collective_compute("AllGather", ALU.bypass, replica_groups=[[0, 1, 2, 3], [4, 5, 6, 7]],
                                        ins=[src.opt()], outs=[dst.opt()])
        S.add_async("pool", fn, cc_sems[k], cc_use[k], reads, writes)

    cc_count = [0]

    def allgather(src, dst):
        cc_count[0] += 1
        n = cc_count[0]

        def fn(e, n=n):
            e.collective_compute("AllGather", ALU.bypass, replica_groups=[[0, 1, 2, 3], [4, 5, 6, 7]],
                                 ins=[src.opt()], outs=[dst.opt()]).then_inc(cc_sem)
            e.wait_ge(cc_sem, n)
            return e.memset(ccdummy[:], 0.0)
        S.add("pool", fn, writes=["ccdummy"])


    def phase_a_moba(li, x_src):
        j = li // 2
        with ExitStack() as st:
            kT = sb("kT", [128, 2, 256], BF16, st)
            vP = sb("vP", [128, 2, 4, 65], BF16, st)
            with ExitStack() as st2:
                mem_prep(li, st2, kT, vP)
                S.flush()
            Win = sb("Win", [128, 8, 2560], BF16, st)
            for cc in range(5):
                S.dma("pool", Win[:, :, cc * 512:(cc + 1) * 512],
                      w_in_moba[j, :, cc * 512:(cc + 1) * 512].rearrange("(c p) n -> p c n", p=128), writes=[("Win", cc)])
            G24 = sb("G24", [128, 24, 64], F32, st)
            cosT = sb("cosT", [128, 32, 32], F32, st)
            sinT = sb("sinT", [128, 32, 32], F32, st)
            S.dma("sp", G24[:], mobagain_in[:, j, :, :], writes=["G24"])
            S.dma("sp", cosT[:], cos_in[:, :, :], writes=["rope"])
            S.dma("sp", sinT[:], sin_in[:, :, :], writes=["rope"])
            KMs = sb("KMs", [128, 6, NB], F32, st)
            KMb = sb("KMb", [128, 6, NB], BF16, st)
            xt = [sb("xt%d" % i, [128, 8, 256], F32, st) for i in range(2)]
            sq = [sb("sq%d" % i, [128, 8, 256], BF16, st) for i in range(2)]
            rs = [sb("rs%d" % i, [128, 256], F32, st) for i in range(2)]
            rstd = [sb("rstd%d" % i, [128, 256], F32, st) for i in range(2)]
            xb = [sb("xb%d" % i, [128, 8, 256], BF16, st) for i in range(2)]
            hf = [sb("hf%d" % i, [128, 2, 1536], F32, st) for i in range(2)]
            sq2 = sb("sq2", [128, 1536], F32, st)
            ss = sb("ss", [128, 24], F32, st)
            rsn = sb("rsn", [128, 24], F32, st)
            qn = sb("qn", [128, 1536], F32, st)
            t1 = sb("t1", [128, 24, 32], F32, st)
            t2 = sb("t2", [128, 24, 32], F32, st)
            t3 = sb("t3", [128, 24, 32], F32, st)
            t4 = sb("t4", [128, 24, 32], F32, st)
            qkb = [sb("qkb%d" % i, [128, 2, 1536], BF16, st) for i in range(2)]
            vb = [[sb("vb%d_%d" % (i, k), [128, 6, 2, 65], BF16, st) for k in range(2)] for i in range(2)]
            qTs = sb("qTs", [128, 6, 256], BF16, st)
            kTs = sb("kTs", [128, 6, 256], BF16, st)
            mat = [mem_attn_tiles(st, i) for i in range(2)]
            for i in range(2):
                for k in range(2):
                    S.add("pool", lambda e, i=i, k=k: e.memset(vb[i][k][:], 1.0), writes=[("vb", i, k)])

            def loads(s):
                p = s % 2
                S.dma("sp", xt[p][:], x_src[:, s * BS:(s + 1) * BS].rearrange("(c p) t -> p c t", p=128),
                      writes=[("xt", p)])

            def frontNa(s):
                p = s % 2
                fm_sq(xt[p], 256, sq[p], "a%d" % p, [("xt", p)])

            def frontNb(s):
                p = s % 2
                xres = [("xt", p)]
                fm_rstd2(256, sq[p], rs[p], rstd[p], 4, "a%d" % p)
                fm_apply(xt[p], 256, gvec[:, 0, li, :], rstd[p], xb[p], "a%d" % p, xres, ("xb", p))

            def frontP(s):
                p = s % 2
                qm = mat[p][0]
                for tt in range(2):
                    for cc in range(5):
                        bk = cc % 4
                        mm(bank(bk), [(xb[p][:, c, tt * 128:(tt + 1) * 128], Win[:, c, cc * 512:(cc + 1) * 512])
                                      for c in range(8)], reads=[("xb", p), ("Win", cc)], writes=[BK(bk)])
                        if cc < 3:
                            S.add("act", lambda e, tt=tt, cc=cc, bk=bk, p=p: e.activation(
                                out=hf[p][:, tt, cc * 512:(cc + 1) * 512], in_=bank(bk), func=AF.Copy),
                                reads=[BK(bk)], writes=[("hf", p, tt, cc)])
                        elif cc == 3:
                            S.add("act", lambda e, tt=tt, bk=bk, p=p: e.activation(
                                out=vb[p][tt][:, 0:4, :, 0:64], in_=bank(bk).rearrange("p (c h d) -> p c h d", c=4, h=2),
                                func=AF.Copy), reads=[BK(bk)], writes=[("vb", p, tt)])
                        else:
                            S.add("act", lambda e, tt=tt, bk=bk, p=p: e.activation(
                                out=vb[p][tt][:, 4:6, :, 0:64],
                                in_=bank(bk)[:, 0:256].rearrange("p (c h d) -> p c h d", c=2, h=2),
                                func=AF.Copy), reads=[BK(bk)], writes=[("vb", p, tt)])
                            S.add("act", lambda e, tt=tt, bk=bk, qm=qm: e.activation(
                                out=qm[:, tt, :], in_=bank(bk)[:, 256:512], func=AF.Copy),
                                reads=[BK(bk)], writes=[("qm", p, tt)])
                    ck, ltl = s // 2, (s % 2) * 2 + tt
                    S.dma("pool", Vsrc[ck * 128:(ck + 1) * 128, :].rearrange("p (c l e) -> p c l e", l=4, c=6)[:, :, ltl, :],
                          vb[p][tt][:].rearrange("p c h e -> p c (h e)"), reads=[("vb", p, tt)], writes=[("Vsrc", ck)])

            def rope(s):
                p = s % 2
                for tt in range(2):
                    A3 = hf[p][:, tt, :].rearrange("p (h d) -> p h d", h=24)
                    hq = [("hf", p, tt, 0), ("hf", p, tt, 1), ("hf", p, tt, 2)]
                    S.add("act", lambda e, tt=tt, p=p: e.activation(out=sq2[:], in_=hf[p][:, tt, :], func=AF.Square),
                          reads=hq, writes=["sq2"])
                    S.add("dve", lambda e: e.tensor_reduce(out=ss[:], in_=sq2[:].rearrange("p (h d) -> p h d", h=24),
                                                           axis=AX.X, op=ALU.add), reads=["sq2"], writes=["ss"])
                    S.add("act", lambda e: e.activation(out=rsn[:], in_=ss[:], func=AF.Ln, scale=1.0 / 64, bias=EPS),
                          reads=["ss"], writes=["rsn"])
                    S.add("act", lambda e: e.activation(out=ss[:], in_=rsn[:], func=AF.Exp, scale=-0.5),
                          reads=["rsn"], writes=["ss"])
                    S.add("dve", lambda e, A3=A3: e.tensor_tensor(
                        out=qn[:].rearrange("p (h d) -> p h d", h=24), in0=A3,
                        in1=ss[:].unsqueeze(2).to_broadcast([128, 24, 64]), op=ALU.mult),
                        reads=hq + ["ss"], writes=["qn"])
                    S.add("pool", lambda e: e.tensor_tensor(out=qn[:], in0=qn[:],
                                                            in1=G24[:].rearrange("p h d -> p (h d)"), op=ALU.mult),
                          reads=["qn", "G24"], writes=["qn"])
                    q3 = qn[:].rearrange("p (h d) -> p h d", h=24)
                    x1, x2 = q3[:, :, 0:32], q3[:, :, 32:64]
                    ti = 2 * s + tt
                    cb = cosT[:, ti, :].unsqueeze(1).to_broadcast([128, 24, 32])
                    sn = sinT[:, ti, :].unsqueeze(1).to_broadcast([128, 24, 32])
                    o3 = qkb[p][:, tt, :].rearrange("p (h d) -> p h d", h=24)
                    S.add("dve", lambda e, x1=x1, cb=cb: e.tensor_tensor(out=t1[:], in0=x1, in1=cb, op=ALU.mult),
                          reads=["qn", "rope"], writes=["t1"])
                    S.add("pool", lambda e, x2=x2, sn=sn: e.tensor_tensor(out=t2[:], in0=x2, in1=sn, op=ALU.mult),
                          reads=["qn", "rope"], writes=["t2"])
                    S.add("dve", lambda e, o3=o3: e.tensor_tensor(out=o3[:, :, 0:32], in0=t1[:], in1=t2[:], op=ALU.subtract),
                          reads=["t1", "t2"], writes=[("qkb", p, tt, 0)])
                    S.add("pool", lambda e, x2=x2, cb=cb: e.tensor_tensor(out=t3[:], in0=x2, in1=cb, op=ALU.mult),
                          reads=["qn", "rope"], writes=["t3"])
                    S.add("dve", lambda e, x1=x1, sn=sn: e.tensor_tensor(out=t4[:], in0=x1, in1=sn, op=ALU.mult),
                          reads=["qn", "rope"], writes=["t4"])
                    S.add("pool", lambda e, o3=o3: e.tensor_tensor(out=o3[:, :, 32:64], in0=t3[:], in1=t4[:], op=ALU.add),
                          reads=["t3", "t4"], writes=[("qkb", p, tt, 1)])

            def back(s):
                p = s % 2
                qkres = [("qkb", p, tt, k) for tt in range(2) for k in range(2)]
                mmg([(bank(5)[:, hp:hp + 1], [(qkb[p][:, tt, 768 + hp * 128:768 + (hp + 1) * 128], ones_bf[:, 0:1])
                                              for tt in range(2)]) for hp in range(6)],
                    reads=qkres + ["ones"], writes=[BK(5)])
                S.add("act", lambda e, s=s: e.activation(out=KMs[:, :, s], in_=bank(5)[:, 0:6], func=AF.Copy, scale=1.0 / BS),
                      reads=[BK(5)], writes=["KMs"])
                for which, dst_t, dram in ((0, qTs, QTd), (1, kTs, KTsrc)):
                    def tpa(e, which=which, p=p):
                        for hp in range(4):
                            for tt in range(2):
                                ins = e.transpose(out=psb[0][:, (hp * 2 + tt) * 128:(hp * 2 + tt + 1) * 128],
                                                  in_=qkb[p][:, tt, which * 768 + hp * 128: which * 768 + (hp + 1) * 128],
                                                  identity=ident[:, :])
                        return ins

                    def tpb(e, which=which, p=p):
                        for hp in range(4, 6):
                            for tt in range(2):
                                ins = e.transpose(out=psb[1][:, ((hp - 4) * 2 + tt) * 128:((hp - 4) * 2 + tt + 1) * 128],
                                                  in_=qkb[p][:, tt, which * 768 + hp * 128: which * 768 + (hp + 1) * 128],
                                                  identity=ident[:, :])
                        return ins
                    S.add("pe", tpa, reads=qkres + ["ident"], writes=[("psb", 0)])
                    S.add("act", lambda e, dst_t=dst_t: e.activation(
                        out=dst_t[:, 0:4, :], in_=psb[0][:, 0:1024].rearrange("p (a b) -> p a b", a=4), func=AF.Copy),
                        reads=[("psb", 0)], writes=[("qkT", which)])
                    S.add("pe", tpb, reads=qkres + ["ident"], writes=[("psb", 1)])
                    S.add("dve", lambda e, dst_t=dst_t: e.tensor_copy(
                        out=dst_t[:, 4:6, :], in_=psb[1][:, 0:512].rearrange("p (a b) -> p a b", a=2)),
                        reads=[("psb", 1)], writes=[("qkT", which)])
                    if which == 0:
                        S.dma("pool", QTd[:, s * BS:(s + 1) * BS].rearrange("(c p) t -> p c t", p=128), dst_t[:],
                              reads=[("qkT", which)], writes=[("QTd", s)])
                    else:
                        ck = s // 2
                        S.dma("pool", KTsrc[ck * 768:(ck + 1) * 768, (s % 2) * BS:(s % 2 + 1) * BS]
                              .rearrange("(c p) t -> p c t", p=128), dst_t[:],
                              reads=[("qkT", which)], writes=[("KTsrc", ck)])
                mem_attn(li, s, p, mat[p], kT, vP, lambda tt, p=p: ("qm", p, tt), 6, "rest")
                if s % 2 == 1:
                    ck = s // 2
                    allgather_async(1 + 2 * ck, KTsrc[ck * 768:(ck + 1) * 768, :], KTall[ck * 3072:(ck + 1) * 3072, :],
                                    [("KTsrc", ck)], [("KTall", ck)])
                    allgather_async(2 + 2 * ck, Vsrc[ck * 128:(ck + 1) * 128, :], Vall[ck * 512:(ck + 1) * 512, :],
                                    [("Vsrc", ck)], [("Vall", ck)])

            loads(0)
            loads(1)
            frontNa(0)
            frontNb(0)
            frontNa(1)
            for i in range(NB + 2):
                if i + 1 < NB:
                    frontNb(i + 1)
                if i + 2 < NB:
                    loads(i + 2)
                if 1 <= i <= NB:
                    rope(i - 1)
                if i < NB:
                    frontP(i)
                if i + 2 < NB:
                    frontNa(i + 2)
                if i >= 2:
                    back(i - 2)
                if i < NB:
                    mem_attn(li, i, i % 2, mat[i % 2], kT, vP, lambda tt, p=i % 2: ("qm", p, tt), 6, "prep")
            S.add("dve", lambda e: e.tensor_copy(out=KMb[:], in_=KMs[:]), reads=["KMs"], writes=["KMb"])
            S.dma("pool", KMsrc.rearrange("(c p) s -> p c s", p=128), KMb[:], reads=["KMb"], writes=["KMsrc"])
            allgather_async(0, KMsrc, KMall, ["KMsrc"], ["KMall"])
            S.flush()

    def phase_b(li):
        with ExitStack() as st:
            KMst = sb("KMst", [128, 6, 4, NB], BF16, st)
            KM = sb("KM", [128, 6, 64], BF16, st)
            candb = sb("candb", [128, NB, 64], F32, st)
            cmask = sb("cmask", [128, 2, 2, 256], BF16, st)
            for r in range(4):
                S.dma("sp", KMst[:, :, r, :], KMall[r * 768:(r + 1) * 768, :].rearrange("(c p) s -> p c s", p=128),
                      writes=["KMst"])
            S.add("dve", lambda e: e.tensor_copy(out=KM[:].rearrange("p c (s r) -> p c r s", r=4), in_=KMst[:]),
                  reads=["KMst"], writes=["KM"])
            S.dma("sp", candb[:], candb_in[:, :, :], writes=["candb"])
            S.dma("pool", cmask[:], cmask_in[:, :, :, :], writes=["cmask"])
            KTsb = [sb("KTsb%d" % i, [128, 64, 256], BF16, st) for i in range(2)]
            Vsb = [sb("Vsb%d" % i, [128, 128, 130], BF16, st) for i in range(2)]
            KTown = [sb("KTown%d" % i, [128, NB, 256], BF16, st) for i in range(2)]
            Vown = [sb("Vown%d" % i, [128, 32, 130], BF16, st) for i in range(2)]
            QT = [sb("QT%d" % i, [128, NB, 256], BF16, st) for i in range(2)]
            PT = [sb("PT%d" % i, [128, 1024], BF16, st) for i in range(3)]
            acc = [sb("acc%d" % i, [128, 4, 65], F32, st) for i in range(2)]
            tmp = [sb("tmp%d" % i, [128, 4, 65], F32, st) for i in range(2)]
            sel = [sb("sel%d" % i, [128, 4, 64], F32, st) for i in range(2)]
            gg = sb("gg", [128, 4, 64], F32, st)
            m8 = sb("m8", [128, 4, 8], F32, st)
            thr = sb("thr", [128, 4], F32, st)
            rc = sb("rc", [128, 4], F32, st)
            otok = sb("otok", [128, 2, 128], BF16, st)
            tokTt = [sb("tokTt%d" % i, [128, 256], BF16, st) for i in range(2)]

            def loads(hp):
                b = hp % 2
                ktv = KTsb[b][:].rearrange("p (c u r) k -> p c r u k", u=2, r=4)
                vsv = Vsb[b][:].rearrange("p (c u r h) e -> p c r u h e", u=2, r=4, h=2)
                for ck in range(8):
                    for r in range(4):
                        S.dma("sp", ktv[:, ck, r],
                              KTall[ck * 3072 + r * 768 + hp * 128: ck * 3072 + r * 768 + (hp + 1) * 128, :]
                              .rearrange("p (u k) -> p u k", u=2), writes=[("KTsb", b)])
                        S.dma("sp", vsv[:, ck, r],
                              Vall[ck * 512 + r * 128: ck * 512 + (r + 1) * 128, hp * 520:(hp + 1) * 520]
                              .rearrange("p (u h e) -> p u h e", u=2, h=2), writes=[("Vsb", b)])
                    S.dma("sp", KTown[b][:, 2 * ck:2 * ck + 2, :],
                          KTsrc[ck * 768 + hp * 128: ck * 768 + (hp + 1) * 128, :].rearrange("p (u k) -> p u k", u=2),
                          writes=[("KTown", b)])
                    S.dma("sp", Vown[b][:, 4 * ck:4 * ck + 4, :],
                          Vsrc[ck * 128:(ck + 1) * 128, hp * 520:(hp + 1) * 520].rearrange("p (l e) -> p l e", l=4),
                          writes=[("Vown", b)])
                S.dma("sp", QT[b][:], QTd[hp * 128:(hp + 1) * 128, :].rearrange("p (s k) -> p s k", k=256),
                      writes=[("QT", b)])

            def prologue(hp, s):
                b, sp_ = hp % 2, s % 2
                for hh in range(2):
                    rows = slice(hh * 64, hh * 64 + 64)
                    mmg([(bank(4 + hh)[:, qt * 64:(qt + 1) * 64],
                          [(QT[b][rows, s, qt * 128:(qt + 1) * 128], KM[rows, hp, :])]) for qt in range(2)],
                        reads=[("QT", b), "KM"], writes=[BK(4 + hh)])
                    S.add("dve", lambda e, hh=hh, s=s: e.tensor_tensor(
                        out=gg[:, 2 * hh:2 * hh + 2, :], in0=bank(4 + hh)[:, 0:128].rearrange("p (a b) -> p a b", a=2),
                        in1=candb[:, s, :].unsqueeze(1).to_broadcast([128, 2, 64]), op=ALU.add),
                        reads=[BK(4 + hh), "candb"], writes=[("gg", hh)])
                for i in range(4):
                    S.add("dve", lambda e, i=i: e.max(out=m8[:, i, :], in_=gg[:, i, :]),
                          reads=[("gg", i // 2)], writes=[("m8", i)])
                S.add("dve", lambda e: e.tensor_scalar_max(out=thr[:], in0=m8[:, :, 2], scalar1=-1.0e29),
                      reads=[("m8", i) for i in range(4)], writes=["thr"])
                S.add("dve", lambda e, sp_=sp_: e.tensor_tensor(
                    out=sel[sp_][:], in0=gg[:], in1=thr[:].unsqueeze(2).to_broadcast([128, 4, 64]), op=ALU.is_ge),
                    reads=[("gg", 0), ("gg", 1), "thr"], writes=[("sel", sp_)])
                S.add("pool", lambda e, sp_=sp_: e.memset(acc[sp_][:], 0.0), writes=[("acc", sp_)])

            SK = [(BK(0), BK(1)), (BK(2), BK(3)), (BK(6), BK(7))]

            def stage1(n, hp, s, jj, own):
                b, par = hp % 2, n % 3
                grp = []
                for hh in range(2):
                    rows = slice(hh * 64, hh * 64 + 64)
                    for half in range(2):
                        kt = KTown[b][rows, s, half * 128:(half + 1) * 128] if own else \
                            KTsb[b][rows, jj, half * 128:(half + 1) * 128]
                        grp.append((ps2[par][:, hh * 512 + half * 256: hh * 512 + (half + 1) * 256],
                                    [(kt, QT[b][rows, s, :])]))
                mmg(grp, reads=[("QT", b), ("KTown", b) if own else ("KTsb", b)], writes=list(SK[par]))
                S.add("act", lambda e, par=par: e.activation(out=PT[par][:], in_=ps2[par][:, :], func=AF.Exp, scale=0.125),
                      reads=list(SK[par]), writes=[("PT", par)])
                if own:
                    S.add("pool", lambda e, par=par: e.tensor_tensor(
                        out=PT[par][:], in0=PT[par][:], in1=cmask[:].rearrange("p a b c -> p (a b c)"), op=ALU.mult),
                        reads=[("PT", par), "cmask"], writes=[("PT", par)])

            def stage2(n, hp, s, jj, own):
                b, par3, par, sp_ = hp % 2, n % 3, n % 2, s % 2
                grp = []
                for hh in range(2):
                    for qt in range(2):
                        i = 2 * hh + qt
                        pcs = []
                        for half in range(2):
                            vv = Vown[b][:, 2 * s + half, hh * 65:(hh + 1) * 65] if own else \
                                Vsb[b][:, 2 * jj + half, hh * 65:(hh + 1) * 65]
                            c0 = hh * 512 + half * 256 + qt * 128
                            pcs.append((PT[par3][:, c0:c0 + 128], vv))
                        grp.append((bank(4 + par)[:, i * 65:(i + 1) * 65], pcs))
                mmg(grp, reads=[("PT", par3), ("Vown", b) if own else ("Vsb", b)], writes=[BK(4 + par)])
                pv = bank(4 + par)[:, 0:260].rearrange("p (a b) -> p a b", a=4)
                if own:
                    S.add("dve", lambda e, par=par, pv=pv: e.tensor_copy(out=tmp[par][:], in_=pv),
                          reads=[BK(4 + par)], writes=[("tmp", par)])
                else:
                    S.add("dve", lambda e, par=par, pv=pv, jj=jj, sp_=sp_: e.tensor_tensor(
                        out=tmp[par][:], in0=pv, in1=sel[sp_][:, :, jj:jj + 1].to_broadcast([128, 4, 65]), op=ALU.mult),
                        reads=[BK(4 + par), ("sel", sp_)], writes=[("tmp", par)])
                S.add("pool", lambda e, par=par, sp_=sp_: e.tensor_tensor(out=acc[sp_][:], in0=acc[sp_][:], in1=tmp[par][:],
                                                                         op=ALU.add),
                      reads=[("tmp", par), ("acc", sp_)], writes=[("acc", sp_)])

            def epilogue(hp, s):
                sp_ = s % 2
                S.add("dve", lambda e, sp_=sp_: e.reciprocal(out=rc[:], in_=acc[sp_][:, :, 64]),
                      reads=[("acc", sp_)], writes=["rc"])
                S.add("dve", lambda e, sp_=sp_: e.tensor_tensor(
                    out=otok[:].rearrange("p q (h d) -> p h q d", h=2),
                    in0=acc[sp_][:, :, 0:64].rearrange("p (h q) d -> p h q d", h=2),
                    in1=rc[:].rearrange("p (h q) -> p h q", h=2).unsqueeze(3).to_broadcast([128, 2, 2, 64]), op=ALU.mult),
                    reads=[("acc", sp_), "rc"], writes=["otok"])

                pbv = bank(4 + sp_).bitcast(BF16)

                def tpo(e, sp_=sp_, pbv=pbv):
                    for qt in range(2):
                        ins = e.transpose(out=pbv[:, qt * 128:(qt + 1) * 128], in_=otok[:, qt, :], identity=ident[:, :])
                    return ins
                S.add("pe", tpo, reads=["otok", "ident"], writes=[BK(4 + sp_)])
                S.add("act", lambda e, sp_=sp_, pbv=pbv: e.activation(out=tokTt[sp_][:], in_=pbv[:, 0:256], func=AF.Copy),
                      reads=[BK(4 + sp_)], writes=[("tokTt", sp_)])
                S.dma("pool", cat[hp * 128:(hp + 1) * 128, s * BS:(s + 1) * BS], tokTt[sp_][:],
                      reads=[("tokTt", sp_)], writes=[("cat", hp, s)])

            iters = [(hp, s, jj, jj == 4 * s + 4) for hp in range(6) for s in range(NB) for jj in range(4 * s + 5)]
            N = len(iters)
            loads(0)
            def back(n):
                hp, s, jj, own = iters[n]
                stage2(n, hp, s, jj, own)
                if own:
                    epilogue(hp, s)

            LOOK = 2
            pending = []
            for n in range(N):
                hp, s, jj, own = iters[n]
                if s == 0 and jj == LOOK + 1 and hp + 1 < 6:
                    loads(hp + 1)
                if jj == 0:
                    prologue(hp, s)
                stage1(n, hp, s, jj, own)
                pending.append(n)
                while len(pending) > LOOK:
                    back(pending.pop(0))
            while pending:
                back(pending.pop(0))
            S.flush()

    def phase_c(li, x_src, pool_layer, pre=None):
        TW = 512
        if pool_layer:
            kp = [(pi, 128 if sub == 0 else 64, 192 * g + 128 * sub) for pi, (g, sub) in enumerate(PIECES)]
            kp += [(8, 128, 768), (9, 128, 896)]
        else:
            kp = [(c, 128, 128 * c) for c in range(8)]
        NP = len(kp)
        with ExitStack() as st:
            Wo = sb("Wo", [128, NP, D], BF16, st)
            for (pi, K, r0) in kp:
                S.dma("pool", Wo[0:K, pi, :], w_out[li, r0:r0 + K, :], writes=["Wo"])
            if pre is not None:
                pre()
            catT = [sb("catT%d" % i, [128, NP, TW], BF16, st) for i in range(2)]
            xt = [sb("cxt%d" % i, [128, 8, TW], F32, st) for i in range(2)]
            sq = sb("csq", [128, 8, TW], BF16, st)
            rs = sb("crs", [128, TW], F32, st)
            rstd = sb("crstd", [128, TW], F32, st)
            xbt = [sb("cxb%d" % i, [128, 8, TW], BF16, st) for i in range(2)]

            def loads(t):
                p = t % 2
                S.dma("sp", catT[p][:], cat[0:NP * 128, t * TW:(t + 1) * TW].rearrange("(c p) t -> p c t", p=128),
                      writes=[("catT", p)])
                S.dma("sp", xt[p][:], x_src[:, t * TW:(t + 1) * TW].rearrange("(c p) t -> p c t", p=128),
                      writes=[("cxt", p, dc) for dc in range(8)])
            def outproj(t):
                p = t % 2
                for dc in range(8):
                    bk = dc % 4
                    mm(bank(bk), [(Wo[0:K, pi, dc * 128:(dc + 1) * 128], catT[p][0:K, pi, :]) for (pi, K, r0) in kp],
                       reads=["Wo", ("catT", p)], writes=[BK(bk)])
                    S.add("dve", lambda e, dc=dc, bk=bk, p=p: e.tensor_tensor(
                        out=xt[p][:, dc, :], in0=bank(bk), in1=xt[p][:, dc, :], op=ALU.add),
                        reads=[BK(bk), ("cxt", p, dc)], writes=[("cxt", p, dc)])
                S.dma("pool", xTs[:, t * TW:(t + 1) * TW].rearrange("(c p) t -> p c t", p=128), xt[p][:],
                      reads=[("cxt", p, dc) for dc in range(8)], writes=[("xTs", t)])

            def normst(t):
                p = t % 2
                xres = [("cxt", p, dc) for dc in range(8)]
                fm_rstd(xt[p], TW, sq, rs, rstd, 5, "c", xres)
                fm_apply(xt[p], TW, gvec[:, 2, li, :], rstd, xbt[p], "c", xres, ("cxb", p))
                S.dma("pool", xb2[:, t * TW:(t + 1) * TW].rearrange("(c p) t -> p c t", p=128), xbt[p][:],
                      reads=[("cxb", p)], writes=[("xb2", t)])

            NT = T // TW
            loads(0)
            outproj(0)
            for t in range(NT):
                if t + 1 < NT:
                    loads(t + 1)
                    outproj(t + 1)
                normst(t)
            S.flush()

    def ffn_w_load(li, hf, W1, W2):
        for c4 in range(4):
            S.dma("pool", W1[:, 2 * c4:2 * c4 + 2, :],
                  w_ff1[li, c4 * 256:(c4 + 1) * 256, hf * 2048:(hf + 1) * 2048].rearrange("(c p) n -> p c n", p=128),
                  writes=[("W1", hf, c4)])
        for c4 in range(4):
            S.dma("pool", W2[:, 4 * c4:4 * c4 + 4, :],
                  w_ff2[li, hf * 2048 + c4 * 512: hf * 2048 + (c4 + 1) * 512, :].rearrange("(f p) n -> p f n", p=128),
                  writes=[("W2", hf, c4)])

    def phase_d(li, last, WA):
        TW = 512
        NT = T // TW
        with ExitStack() as st:
            WB = (sb("W1b", [128, 8, 2048], BF16, st), sb("W2b", [128, 16, D], BF16, st))
            ffn_w_load(li, 1, *WB)
            xbt = [sb("dxb%d" % i, [128, 8, TW], BF16, st) for i in range(2)]
            xt = sb("dxt", [128, 8, TW], F32, st)
            hm = [sb("hm%d" % i, [128, 16, TW], BF16, st) for i in range(2)]
            rt = [sb("rt%d" % i, [128, TW], BF16, st) for i in range(3)]
            for hf in range(2):
                dst = out_T if (last and hf == 1) else xTs
                W1, W2 = WA if hf == 0 else WB
                w1res = [("W1", hf, c4) for c4 in range(4)]
                w2res = [("W2", hf, c4) for c4 in range(4)]

                def loads_b(t):
                    p = t % 2
                    S.dma("sp", xbt[p][:], xb2[:, t * TW:(t + 1) * TW].rearrange("(c p) t -> p c t", p=128),
                          writes=[("dxb", p)])

                def loads_x(t):
                    S.dma("sp", xt[:], xTs[:, t * TW:(t + 1) * TW].rearrange("(c p) t -> p c t", p=128),
                          writes=[("dxt", dc) for dc in range(8)])

                def ffn1(t):
                    p = t % 2
                    for fc in range(16):
                        bk = fc % 3
                        ri = fc % 3
                        mm(bank(bk), [(W1[:, c, fc * 128:(fc + 1) * 128], xbt[p][:, c, :]) for c in range(8)],
                           reads=w1res + [("dxb", p)], writes=[BK(bk)])
                        S.add("act", lambda e, bk=bk, ri=ri: e.activation(out=rt[ri][:], in_=bank(bk), func=AF.Relu),
                              reads=[BK(bk)], writes=[("rt", ri)])
                        eng = "dve" if fc % 2 == 0 else "pool"
                        S.add(eng, lambda e, fc=fc, ri=ri, p=p: e.tensor_tensor(
                            out=hm[p][:, fc, :], in0=rt[ri][:], in1=rt[ri][:], op=ALU.mult),
                            reads=[("rt", ri)], writes=[("hm", p, fc)])

                def ffn2(t):
                    p = t % 2
                    for dc in range(8):
                        bk = 3 + dc % 3
                        mm(bank(bk), [(W2[:, fc, dc * 128:(dc + 1) * 128], hm[p][:, fc, :]) for fc in range(16)],
                           reads=w2res + [("hm", p, fc) for fc in range(16)], writes=[BK(bk)])
                        S.add("dve", lambda e, dc=dc, bk=bk: e.tensor_tensor(
                            out=xt[:, dc, :], in0=bank(bk), in1=xt[:, dc, :], op=ALU.add),
                            reads=[BK(bk), ("dxt", dc)], writes=[("dxt", dc)])
                    S.dma("pool", dst[:, t * TW:(t + 1) * TW].rearrange("(c p) t -> p c t", p=128), xt[:],
                          reads=[("dxt", dc) for dc in range(8)], writes=[("xdst", t)])

                loads_b(0)
                ffn1(0)
                for t in range(NT):
                    if t + 1 < NT:
                        loads_b(t + 1)
                    loads_x(t)
                    if t + 1 < NT:
                        ffn1(t + 1)
                    ffn2(t)
                S.flush()

    for li in range(n_layers):
        x_src = xT_in if li == 0 else xTs
        if li % 2 == 0:
            phase_a_pool(li, x_src)
            if stopped[0] or chk("A%d" % li):
                break
        else:
            phase_a_moba(li, x_src)
            if chk("A%d" % li):
                break
            phase_b(li)
            if chk("B%d" % li):
                break
        with ExitStack() as stw:
            WA = (sb("W1a", [128, 8, 2048], BF16, stw), sb("W2a", [128, 16, D], BF16, stw))
            phase_c(li, x_src, li % 2 == 0, pre=lambda li=li, WA=WA: ffn_w_load(li, 0, *WA))
            phase_d(li, li == n_layers - 1, WA)
    es.close()
    return nc


POOL_WINDOWS = (2, 4, 8, 16)


def _mix_mats(r):
    m = np.zeros((128, 5, 4, 256), np.float32)
    for g, w in enumerate(POOL_WINDOWS):
        full = np.zeros((16 + 256, 256), np.float32)
        fullf = np.zeros((16 + 256, 256), np.float32)
        for t in range(256):
            for u in range(t - w + 1, t + 1):
                full[16 + u, t] += 1.0 / w
                if u >= 0:
                    fullf[16 + u, t] += 1.0 / min(t + 1, w)
            full[16 + t, t] -= 1.0
            fullf[16 + t, t] -= 1.0
        m[:, 0, g, :] = full[16:144]
        m[:, 1, g, :] = full[144:272]
        m[:, 2, g, :] = fullf[16:144] if r == 0 else full[16:144]
        m[0:16, 3, g, :] = full[0:16]
        rr = (r - 1) % 4
        v = 0 if r > 0 else 1
        q = rr * 2 + v
        m[q * 16:(q + 1) * 16, 4, g, :] = full[0:16]
    return m


def _host_prep(inputs):
    x = np.asarray(inputs["x"], np.float32)
    mem = np.asarray(inputs["mem"], np.float32)
    B, S_, _ = x.shape
    nblk = S_ // BS
    shared = {}
    for k in ("w_in_pool", "w_pool_group", "w_in_moba", "w_mem_kv", "w_out", "w_ff1", "w_ff2"):
        shared[k] = np.ascontiguousarray(np.asarray(inputs[k], np.float32))
    g3 = np.stack([np.asarray(inputs[k], np.float32) for k in ("g_mix", "g_mem", "g_mlp")])
    shared["gvec"] = np.ascontiguousarray(g3.reshape(3, 4, 8, 128).transpose(3, 0, 1, 2))
    ps = np.asarray(inputs["pool_scale"], np.float32)
    pscale = np.zeros((128, 2, 8), np.float32)
    for j in range(2):
        for g in range(4):
            pscale[:, j, 2 * g] = ps[j, 192 * g:192 * g + 128]
            pscale[0:64, j, 2 * g + 1] = ps[j, 192 * g + 128:192 * g + 192]
    shared["pscale"] = pscale
    mg = np.stack([np.asarray(inputs["mem_q_gain"], np.float32), np.asarray(inputs["mem_k_gain"], np.float32)], 1)
    shared["memgain"] = np.ascontiguousarray(np.broadcast_to(mg[None], (128, 4, 2, 64)))
    qg = np.asarray(inputs["moba_q_gain"], np.float32)
    kg = np.asarray(inputs["moba_k_gain"], np.float32)
    mb = np.concatenate([np.repeat(qg[:, None, :], 12, 1), np.repeat(kg[:, None, :], 12, 1)], 1)
    shared["mobagain"] = np.ascontiguousarray(np.broadcast_to(mb[None], (128, 2, 24, 64)))
    shared["ident"] = np.eye(128, dtype=np.float32)
    cm = np.zeros((128, 2, 2, 256), np.float32)
    kk = np.arange(128)
    for half in range(2):
        cm[:, :, half, :] = ((half * 128 + kk)[:, None] <= np.arange(256)[None, :])[:, None, :]
    shared["cmask"] = cm
    inv = 10000.0 ** (-np.arange(0, 64, 2, dtype=np.float32) / 64)
    in_maps = []
    for c in range(NCORES):
        b, r = c // 4, c % 4
        xb_ = x[b].reshape(nblk, BS, D)
        mine = xb_[r::4]
        m = dict(shared)
        m["xT"] = np.ascontiguousarray(mine.reshape(T, D).T)
        halo = np.zeros((NB, 16, D), np.float32)
        for s in range(NB):
            gb = 4 * s + r
            if gb > 0:
                halo[s] = xb_[gb - 1, BS - 16:]
        m["xhT"] = np.ascontiguousarray(halo.reshape(NB * 16, D).T)
        m["memT"] = np.ascontiguousarray(mem[b].T)
        m["mmix"] = _mix_mats(r)
        pos = (np.arange(T) // BS * 4 + r) * BS + np.arange(T) % BS
        ang = pos.astype(np.float32)[:, None] * inv[None, :]
        m["cosT"] = np.ascontiguousarray(np.cos(ang).astype(np.float32).reshape(32, 128, 32).transpose(1, 0, 2))
        m["sinT"] = np.ascontiguousarray(np.sin(ang).astype(np.float32).reshape(32, 128, 32).transpose(1, 0, 2))
        cb = np.zeros((NB, 64), np.float32)
        for s in range(NB):
            cb[s, 4 * s + r:] = NEGB
        m["candb"] = np.ascontiguousarray(np.broadcast_to(cb[None], (128, NB, 64)))
        in_maps.append(m)
    return in_maps


_PROG_CACHE = {}


def kernel(_n_layers=4, _cores=NCORES, **inputs):
    in_maps = _host_prep(inputs)[:_cores]
    if _n_layers not in _PROG_CACHE:
        _PROG_CACHE[_n_layers] = build_program(_n_layers)
    nc = _PROG_CACHE[_n_layers]
    res = run_bass_kernel_spmd(nc, in_maps, core_ids=list(range(_cores)))
    x = inputs["x"]
    B, S_, _ = x.shape
    nblk = S_ // BS
    out = np.full((B, nblk, BS, D), np.nan, np.float32)
    for c in range(_cores):
        b, r = c // 4, c % 4
        o = np.asarray(res.results[c]["outT"], np.float32)
        out[b, r::4] = o.T.reshape(NB, BS, D)
    return out.reshape(B, S_, D)
```
